# Optimizing a Trainium2 kernel written in Bass

```python
import math
import jax, jax.numpy as jnp
from jax import lax
import numpy as np

D_MODEL = 1024
BATCH = 4
SEQ = 4096
DEPTH = 2
DEC_BATCH = 8
DEC_SEQ = 4096
PAST_LEN = 128

FNET_GROUP_DIM = 64
FNET_GROUPS = 6
FNET_W = FNET_GROUPS * FNET_GROUP_DIM
S5_GROUP_DIM = 16
S5_GROUPS = 24
S5_W = S5_GROUPS * S5_GROUP_DIM
S5_STATE = 64
S5_DT_MIN = 0.001
S5_DT_MAX = 0.1
MLA_HEADS = 16
QK_NOPE = 64
QK_ROPE = 32
V_DIM = 64
Q_LORA = 384
KV_LORA = 256
ROPE_BASE = 10000.0
Q_BLOCK = 128
N_BRANCH = 3
D_FF = 4 * D_MODEL
EPS = 1e-6

OFF_FNET = 0
OFF_S5 = OFF_FNET + FNET_W
OFF_Q = OFF_S5 + S5_W
OFF_KV = OFF_Q + Q_LORA
OFF_KR = OFF_KV + KV_LORA
OFF_GATE = OFF_KR + QK_ROPE
IN_W = OFF_GATE + N_BRANCH * D_MODEL

kernel_name = "hybrid_fnet_s5_mla_encoder"


def rms_norm(x, g):
    x32 = x.astype(jnp.float32)
    y = x32 * lax.rsqrt(jnp.mean(x32 * x32, axis=-1, keepdims=True) + EPS)
    return (y * g.astype(jnp.float32)).astype(x.dtype)


def fourier_mix(u):
    b, l, _ = u.shape
    ug = u.astype(jnp.float32).reshape(b, l, FNET_GROUPS, FNET_GROUP_DIM)
    f = jnp.fft.fftn(ug, axes=(1, 3), norm="ortho")
    return jnp.real(f).reshape(b, l, FNET_W).astype(u.dtype)


def s5_scan(u, lam_re, lam_im, log_dt, b_re, b_im, c_re, c_im, reverse):
    f32 = jnp.float32
    lam = lax.complex(lam_re.astype(f32), lam_im.astype(f32))
    dt = jnp.exp(log_dt.astype(f32))[:, None]
    lam_bar = jnp.exp(lam * dt)
    b_bar = ((lam_bar - 1.0) / lam)[..., None] * lax.complex(b_re.astype(f32), b_im.astype(f32))
    bu = jnp.einsum('blgp,gnp->blgn', u.astype(jnp.complex64), b_bar)
    a = jnp.broadcast_to(lam_bar, bu.shape)

    def combine(e1, e2):
        a1, h1 = e1
        a2, h2 = e2
        return a1 * a2, a2 * h1 + h2

    _, h = lax.associative_scan(combine, (a, bu), axis=1, reverse=reverse)
    cmat = lax.complex(c_re.astype(f32), c_im.astype(f32))
    return jnp.real(jnp.einsum('blgn,gpn->blgp', h, cmat))


def s5_branch(u, lam_re, lam_im, log_dt, b_re, b_im, c_re, c_im, d_skip, w_glu):
    bsz, l, _ = u.shape
    u32 = u.astype(jnp.float32)
    ug = u32.reshape(bsz, l, S5_GROUPS, S5_GROUP_DIM)
    y_f = s5_scan(ug, lam_re[0], lam_im[0], log_dt[0], b_re[0], b_im[0], c_re[0], c_im[0], False)
    y_b = s5_scan(ug, lam_re[1], lam_im[1], log_dt[1], b_re[1], b_im[1], c_re[1], c_im[1], True)
    y = (y_f + y_b).reshape(bsz, l, S5_W) + d_skip.astype(jnp.float32) * u32
    y = jax.nn.gelu(y).astype(u.dtype)
    h = y @ w_glu
    return h[..., :S5_W] * jax.nn.sigmoid(h[..., S5_W:])


def apply_rope(x):
    l = x.shape[1]
    half = QK_ROPE // 2
    inv = ROPE_BASE ** (-jnp.arange(half, dtype=jnp.float32) / half)
    ang = jnp.arange(l, dtype=jnp.float32)[:, None] * inv[None, :]
    cos = jnp.cos(ang)[None, :, None, :]
    sin = jnp.sin(ang)[None, :, None, :]
    x32 = x.astype(jnp.float32)
    x1, x2 = x32[..., :half], x32[..., half:]
    return jnp.concatenate([x1 * cos - x2 * sin, x1 * sin + x2 * cos], axis=-1).astype(x.dtype)


def mla_branch(c_q, c_kv, k_rope, g_q, w_qb, g_kv, w_kvb, w_o):
    b, l, _ = c_q.shape
    q = (rms_norm(c_q, g_q) @ w_qb).reshape(b, l, MLA_HEADS, QK_NOPE + QK_ROPE)
    q = jnp.concatenate([q[..., :QK_NOPE], apply_rope(q[..., QK_NOPE:])], axis=-1)
    kv = (rms_norm(c_kv, g_kv) @ w_kvb).reshape(b, l, MLA_HEADS, QK_NOPE + V_DIM)
    k_pe = jnp.broadcast_to(apply_rope(k_rope[:, :, None, :]), (b, l, MLA_HEADS, QK_ROPE))
    k32 = jnp.concatenate([kv[..., :QK_NOPE], k_pe], axis=-1).astype(jnp.float32)
    v32 = kv[..., QK_NOPE:].astype(jnp.float32)
    scale = (QK_NOPE + QK_ROPE) ** -0.5
    q_blocks = q.reshape(b, l // Q_BLOCK, Q_BLOCK, MLA_HEADS, QK_NOPE + QK_ROPE).transpose(1, 0, 2, 3, 4)

    def attend(q_blk):
        s = jnp.einsum('bqhd,bkhd->bhqk', q_blk.astype(jnp.float32), k32) * scale
        p = jax.nn.softmax(s, axis=-1)
        return jnp.einsum('bhqk,bkhd->bqhd', p, v32)

    o = lax.map(attend, q_blocks)
    o = o.transpose(1, 0, 2, 3, 4).reshape(b, l, MLA_HEADS * V_DIM).astype(c_q.dtype)
    return o @ w_o


def encoder_layer(x, g_mix, w_in, w_fnet, s5_lam_re, s5_lam_im, s5_log_dt, s5_b_re, s5_b_im,
                  s5_c_re, s5_c_im, s5_d, w_glu, w_s5, g_q, w_qb, g_kv, w_kvb, w_o_mla,
                  w_out, g_mlp, w_up, w_down):
    b, l, _ = x.shape
    h = rms_norm(x, g_mix)
    z = h @ w_in
    y_a = fourier_mix(z[..., OFF_FNET:OFF_S5]) @ w_fnet
    y_b = s5_branch(z[..., OFF_S5:OFF_Q], s5_lam_re, s5_lam_im, s5_log_dt, s5_b_re, s5_b_im,
                    s5_c_re, s5_c_im, s5_d, w_glu) @ w_s5
    y_c = mla_branch(z[..., OFF_Q:OFF_KV], z[..., OFF_KV:OFF_KR], z[..., OFF_KR:OFF_GATE],
                     g_q, w_qb, g_kv, w_kvb, w_o_mla)
    gates = jax.nn.sigmoid(z[..., OFF_GATE:].astype(jnp.float32)).reshape(b, l, N_BRANCH, D_MODEL)
    merged = (gates[:, :, 0] * y_a.astype(jnp.float32)
              + gates[:, :, 1] * y_b.astype(jnp.float32)
              + gates[:, :, 2] * y_c.astype(jnp.float32)).astype(x.dtype)
    x = x + merged @ w_out
    h = rms_norm(x, g_mlp)
    x = x + jnp.square(jax.nn.relu(h @ w_up)) @ w_down
    return x


def trunk(x, g_mix, w_in, w_fnet, s5_lam_re, s5_lam_im, s5_log_dt, s5_b_re, s5_b_im,
          s5_c_re, s5_c_im, s5_d, w_glu, w_s5, g_q, w_qb, g_kv, w_kvb, w_o_mla,
          w_out, g_mlp, w_up, w_down, g_final):
    for i in range(DEPTH):
        x = encoder_layer(x, g_mix[i], w_in[i], w_fnet[i], s5_lam_re[i], s5_lam_im[i], s5_log_dt[i],
                          s5_b_re[i], s5_b_im[i], s5_c_re[i], s5_c_im[i], s5_d[i], w_glu[i], w_s5[i],
                          g_q[i], w_qb[i], g_kv[i], w_kvb[i], w_o_mla[i], w_out[i], g_mlp[i],
                          w_up[i], w_down[i])
    return rms_norm(x, g_final)


def setup_inputs(seed: int = 0) -> dict:
    key = jax.random.key(seed)
    ks = jax.random.split(key, 32)
    f32 = jnp.float32

    def nrm(k, shape, scale):
        return jax.random.normal(k, shape, f32) * scale

    def gain(k, shape):
        return 1.0 + 0.02 * jax.random.normal(k, shape, f32)

    G, N, P = S5_GROUPS, S5_STATE, S5_GROUP_DIM
    n_idx = jnp.arange(N, dtype=f32)
    inp = {
        "x_prompt": nrm(ks[0], (BATCH, SEQ, D_MODEL), 1.0),
        "x_sample": nrm(ks[1], (DEC_BATCH, DEC_SEQ, D_MODEL), 1.0),
        "g_mix": gain(ks[2], (DEPTH, D_MODEL)),
        "w_in": nrm(ks[3], (DEPTH, D_MODEL, IN_W), D_MODEL ** -0.5),
        "w_fnet": nrm(ks[4], (DEPTH, FNET_W, D_MODEL), FNET_W ** -0.5),
        "s5_lam_re": -0.5 + 0.01 * jax.random.normal(ks[5], (DEPTH, 2, G, N), f32),
        "s5_lam_im": math.pi * n_idx + 0.01 * jax.random.normal(ks[6], (DEPTH, 2, G, N), f32),
        "s5_log_dt": jax.random.uniform(ks[7], (DEPTH, 2, G), f32, math.log(S5_DT_MIN), math.log(S5_DT_MAX)),
        "s5_b_re": nrm(ks[8], (DEPTH, 2, G, N, P), (2.0 * P) ** -0.5),
        "s5_b_im": nrm(ks[9], (DEPTH, 2, G, N, P), (2.0 * P) ** -0.5),
        "s5_c_re": nrm(ks[10], (DEPTH, 2, G, P, N), (2.0 * N) ** -0.5),
        "s5_c_im": nrm(ks[11], (DEPTH, 2, G, P, N), (2.0 * N) ** -0.5),
        "s5_d": nrm(ks[12], (DEPTH, S5_W), 1.0),
        "w_glu": nrm(ks[13], (DEPTH, S5_W, 2 * S5_W), S5_W ** -0.5),
        "w_s5": nrm(ks[14], (DEPTH, S5_W, D_MODEL), S5_W ** -0.5),
        "g_q": gain(ks[15], (DEPTH, Q_LORA)),
        "w_qb": nrm(ks[16], (DEPTH, Q_LORA, MLA_HEADS * (QK_NOPE + QK_ROPE)), Q_LORA ** -0.5),
        "g_kv": gain(ks[17], (DEPTH, KV_LORA)),
        "w_kvb": nrm(ks[18], (DEPTH, KV_LORA, MLA_HEADS * (QK_NOPE + V_DIM)), KV_LORA ** -0.5),
        "w_o_mla": nrm(ks[19], (DEPTH, MLA_HEADS * V_DIM, D_MODEL), (MLA_HEADS * V_DIM) ** -0.5),
        "w_out": nrm(ks[20], (DEPTH, D_MODEL, D_MODEL), D_MODEL ** -0.5),
        "g_mlp": gain(ks[21], (DEPTH, D_MODEL)),
        "w_up": nrm(ks[22], (DEPTH, D_MODEL, D_FF), D_MODEL ** -0.5),
        "w_down": nrm(ks[23], (DEPTH, D_FF, D_MODEL), D_FF ** -0.5),
        "g_final": gain(ks[24], (D_MODEL,)),
    }
    return inp


def reference(x_prompt, x_sample, g_mix, w_in, w_fnet, s5_lam_re, s5_lam_im, s5_log_dt, s5_b_re,
              s5_b_im, s5_c_re, s5_c_im, s5_d, w_glu, w_s5, g_q, w_qb, g_kv, w_kvb, w_o_mla,
              w_out, g_mlp, w_up, w_down, g_final):
    params = (g_mix, w_in, w_fnet, s5_lam_re, s5_lam_im, s5_log_dt, s5_b_re, s5_b_im, s5_c_re,
              s5_c_im, s5_d, w_glu, w_s5, g_q, w_qb, g_kv, w_kvb, w_o_mla, w_out, g_mlp,
              w_up, w_down, g_final)
    y_prompt = trunk(x_prompt, *params)
    y_sample = trunk(x_sample, *params)
    return (y_prompt, y_sample)
```

```python
import math
import os
from contextlib import ExitStack

import numpy as np
import ml_dtypes

import concourse.bass as bass
import concourse.mybir as mybir
from concourse.bass_utils import run_bass_kernel_spmd

F32 = mybir.dt.float32
BF16 = mybir.dt.bfloat16
I32 = mybir.dt.int32
AF = mybir.ActivationFunctionType
ALU = mybir.AluOpType

D = 1024
FW_ = 384
S5W = 384
QL = 384
KVL = 256
NH = 16
OFF_FNET = 0
OFF_S5 = 384
OFF_Q = 768
OFF_KV = 1152
OFF_KR = 1408
OFF_GATE = 1440
IN_W = 4512
DFF = 4096
EPS = 1e-6
TST = 64
SCALE = 96 ** -0.5

SEM_ROLL = 30000
S_STOP = 0
NDSEM = 6


class Buf:
    __slots__ = ("w", "r", "x")

    def __init__(self, x=False):
        self.w = {}
        self.r = {}
        self.x = x


def PB():
    return Buf(True)


class Eng:
    def __init__(self, name, h):
        self.name = name
        self.h = h
        self.sem = None
        self.semkey = None
        self.cnt = 0
        self.own = set()
        self.waited = {}
        self.dsems = []
        self.dcnt = 0
        self.nins = 0
        self.nwait = 0
        self.last = {}


class FW:
    def __init__(self, nc, stack):
        self.nc = nc
        self.stack = stack
        self.semtab = {}
        self.nsem = 0
        self.pe = Eng("pe", nc.tensor)
        self.act = Eng("act", nc.scalar)
        self.dve = Eng("dve", nc.vector)
        self.pool = Eng("pool", nc.gpsimd)
        self.sp = Eng("sp", nc.sync)
        self.engs = [self.pe, self.act, self.dve, self.pool, self.sp]
        for e in self.engs:
            self._newsem(e)
        for e in (self.sp, self.pool, self.act):
            for i in range(NDSEM):
                e.dsems.append(self._mksem(f"d_{e.name}_{i}"))

    def _mksem(self, name):
        s = self.stack.enter_context(self.nc.semaphore(name))
        key = self.nsem
        self.nsem += 1
        self.semtab[key] = s
        return key

    def _newsem(self, e):
        key = self._mksem(f"c_{e.name}_{len(e.own)}")
        e.sem = self.semtab[key]
        e.semkey = key
        e.cnt = 0
        e.own.add(key)

    def _wait(self, e, deps):
        for key, val in deps.items():
            if e.waited.get(key, 0) >= val:
                continue
            e.h.wait_ge(self.semtab[key], val)
            e.waited[key] = val
            e.nwait += 1

    def op(self, e, fn, reads=(), writes=(), dma=False):
        deps = {}
        own = e.own
        ispe = e is self.pe
        for b in reads:
            for k, v in b.w.items():
                if ispe and k in own:
                    continue
                if deps.get(k, 0) < v:
                    deps[k] = v
            if b.x:
                for k, v in b.r.items():
                    if k in own:
                        continue
                    if deps.get(k, 0) < v:
                        deps[k] = v
        for b in writes:
            for k, v in b.w.items():
                if k in own:
                    continue
                if deps.get(k, 0) < v:
                    deps[k] = v
            for k, v in b.r.items():
                if k in own:
                    continue
                if deps.get(k, 0) < v:
                    deps[k] = v
        if dma:
            slot = e.dcnt % NDSEM
            use = e.dcnt // NDSEM
            key = e.dsems[slot]
            if use > 0 and deps.get(key, 0) < 16 * use:
                deps[key] = 16 * use
            self._wait(e, deps)
            ins = fn(e.h)
            ins.then_inc(self.semtab[key], 16)
            e.dcnt += 1
            ev = (key, 16 * (use + 1))
        else:
            self._wait(e, deps)
            if e.cnt >= SEM_ROLL:
                self._newsem(e)
            ins = fn(e.h)
            ins.then_inc(e.sem, 1)
            e.cnt += 1
            ev = (e.semkey, e.cnt)
        e.nins += 1
        k, v = ev
        e.last[k] = v
        for b in reads:
            if b.r.get(k, 0) < v:
                b.r[k] = v
        for b in writes:
            if b.w.get(k, 0) < v:
                b.w[k] = v
        return ev

    def dma(self, e, out, in_, reads=(), writes=()):
        return self.op(e, lambda h: h.dma_start(out=out, in_=in_), reads, writes, dma=True)

    def barrier(self):
        deps = {}
        for e in self.engs:
            for k, v in e.last.items():
                if deps.get(k, 0) < v:
                    deps[k] = v
        for e in self.engs:
            d = {k: v for k, v in deps.items() if k not in e.own}
            self._wait(e, d)

    def finish(self, bufs):
        deps = {}
        for b in bufs:
            for k, v in b.w.items():
                if deps.get(k, 0) < v:
                    deps[k] = v
        self._wait(self.sp, deps)


def build_program(L, NS, DEPTH, dbg=False, stop_after=None):
    nc = bass.Bass("TRN2", target_bir_lowering=False)
    NTC = L // 128
    NB = L // 512
    NC_ = L // TST
    NC2 = NS * NC_
    uid = [0]

    def dram(name, shape, dt, kind="Internal"):
        return nc.dram_tensor(name, list(shape), dt, kind=kind).ap()

    def din(name, shape, dt=F32):
        return dram(name, shape, dt, kind="ExternalInput")

    x_in = din("x", [NS, L, D])
    g_mix = din("g_mix", [DEPTH, D])
    w_in = din("w_in", [DEPTH, D, IN_W])
    w_fnet = din("w_fnet", [DEPTH, FW_, D])
    lam_re = din("s5_lam_re", [DEPTH, 2, 24, 64])
    lam_im = din("s5_lam_im", [DEPTH, 2, 24, 64])
    log_dt = din("s5_log_dt", [DEPTH, 2, 24])
    b_re = din("s5_b_re", [DEPTH, 2, 24, 64, 16])
    b_im = din("s5_b_im", [DEPTH, 2, 24, 64, 16])
    c_re = din("s5_c_re", [DEPTH, 2, 24, 16, 64])
    c_im = din("s5_c_im", [DEPTH, 2, 24, 16, 64])
    s5_d = din("s5_d", [DEPTH, S5W])
    w_glu = din("w_glu", [DEPTH, S5W, 2 * S5W])
    w_s5 = din("w_s5", [DEPTH, S5W, D])
    g_q = din("g_q", [DEPTH, QL])
    w_qb = din("w_qb", [DEPTH, QL, NH * 96])
    g_kv = din("g_kv", [DEPTH, KVL])
    w_kvb = din("w_kvb", [DEPTH, KVL, NH * 128])
    w_o = din("w_o_mla", [DEPTH, D, D])
    w_out = din("w_out", [DEPTH, D, D])
    g_mlp = din("g_mlp", [DEPTH, D])
    w_up = din("w_up", [DEPTH, D, DFF])
    w_down = din("w_down", [DEPTH, DFF, D])
    g_final = din("g_final", [D])
    c_identf = din("c_identf", [128, 128])
    c_jswap = din("c_jswap", [128, 128])
    c_misc = din("c_misc", [128, 16])
    c_dft = din("c_dft", [L, 2, L], BF16)
    c_c64 = din("c_c64", [128, 2, 128], BF16)
    c_rope = din("c_rope", [2, 32, L])

    okind = "ExternalOutput"
    y_out = dram("y", [NS, L, D], F32, kind=okind)
    skind = okind if dbg else "Internal"
    zf_d = dram("zf", [NS, L, FW_], BF16, kind=skind)
    zs5T_d = dram("zs5T", [NS, S5W, L], BF16, kind=skind)
    cqnT_d = dram("cqnT", [NS, QL, L], BF16, kind=skind)
    ckvnT_d = dram("ckvnT", [NS, KVL, L], BF16, kind=skind)
    kpeT_d = dram("kpeT", [NS, 32, L], BF16, kind=skind)
    gates_d = dram("gates", [NS, L, 3 * D], BF16, kind=skind)
    fmT_d = dram("fmT", [NS, FW_, L], BF16, kind=skind)
    gluT_d = dram("gluT", [NS, S5W, L], BF16, kind=skind)
    ygT_d = dram("ygT", [NS, S5W, L], BF16, kind=skind)
    o_d = dram("o_att", [NS, L, D], BF16, kind=skind)
    xa_d = dram("xa", [NS, L, D], F32, kind=skind)
    xb_d = dram("xb", [NS, L, D], F32, kind=skind)

    mkb = lambda: [Buf() for _ in range(NS)]
    b_zf, b_zs5T, b_cqnT, b_ckvnT, b_kpeT, b_gates = mkb(), mkb(), mkb(), mkb(), mkb(), mkb()
    b_fmT, b_gluT, b_o, b_xa, b_xb, b_y = mkb(), mkb(), mkb(), mkb(), mkb(), mkb()
    b_const = Buf()
    b_ygT = mkb()

    with ExitStack() as top:
        fw = FW(nc, top)
        pe, act, dve, pool, sp = fw.pe, fw.act, fw.dve, fw.pool, fw.sp

        def T(st, shape, dt, name="t"):
            uid[0] += 1
            return st.enter_context(nc.sbuf_tensor(f"{name}_{uid[0]}", list(shape), dt))

        def P(st, shape, dt, name="p"):
            uid[0] += 1
            return st.enter_context(nc.psum_tensor(f"{name}_{uid[0]}", list(shape), dt))

        identf = T(top, [128, 128], F32, "identf")
        identb = T(top, [128, 128], BF16, "identb")
        jswap = T(top, [128, 128], F32, "jswap")
        misc = T(top, [128, 16], F32, "misc")
        stg = [T(top, [128, 1152], F32, "stg") for _ in range(2)]
        b_stg = [Buf(), Buf()]
        b_id = Buf()
        fw.dma(sp, identf[:], c_identf, writes=[b_id])
        fw.dma(sp, jswap[:], c_jswap, writes=[b_id])
        fw.dma(sp, misc[:], c_misc, writes=[b_id])
        fw.op(dve, lambda h: h.tensor_copy(identb[:], identf[:]), [b_id], [b_id])
        stgi = [0]

        def load_w(dst, src, bdst, n):
            c0 = 0
            while c0 < n:
                cw = min(1152, n - c0)
                i = stgi[0] % 2
                stgi[0] += 1
                fw.dma(sp, stg[i][:, 0:cw], src[:, c0:c0 + cw], writes=[b_stg[i]])
                eng = pool if (stgi[0] % 2) else dve
                fw.op(eng, lambda h: h.tensor_copy(dst[:, c0:c0 + cw], stg[i][:, 0:cw]), [b_stg[i]], [bdst])
                c0 += cw

        def load_w_gen(dst, src, bdst, n):
            c0 = 0
            while c0 < n:
                cw = min(1152, n - c0)
                i = stgi[0] % 2
                stgi[0] += 1
                fw.dma(sp, stg[i][:, 0:cw], src[:, c0:c0 + cw], writes=[b_stg[i]])
                eng = pool if (stgi[0] % 2) else dve
                fw.op(eng, lambda h: h.tensor_copy(dst[:, c0:c0 + cw], stg[i][:, 0:cw]), [b_stg[i]], [bdst])
                c0 += cw
                yield

        def evac(eng, out, in_, reads, writes):
            if eng is act:
                return fw.op(act, lambda h: h.activation(out, in_, AF.Copy), reads, writes)
            return fw.op(eng, lambda h: h.tensor_copy(out, in_), reads, writes)

        def rstd_from_ss(st_tiles, ss, n, width, reads_b, b_out):
            fw.op(dve, lambda h: h.tensor_scalar(ss[:, 0:n], ss[:, 0:n], 1.0 / width, EPS, ALU.mult, ALU.add), [reads_b], [b_out])
            fw.op(act, lambda h: h.activation(ss[:, 0:n], ss[:, 0:n], AF.Sqrt), [b_out], [b_out])
            fw.op(dve, lambda h: h.reciprocal(ss[:, 0:n], ss[:, 0:n]), [b_out], [b_out])

        def phase_A(l, seqs):
            with ExitStack() as ph:
                w = T(ph, [128, 8, IN_W], BF16, "w_in"); bw = Buf()
                for c in range(8):
                    load_w(w[:, c, :], w_in[l, c * 128:(c + 1) * 128, :], bw, IN_W)
                wsw = T(ph, [128, 8, 96], BF16, "wsw"); bwsw = Buf()
                fw.op(pool, lambda h: h.tensor_copy(wsw[:, :, 0:64], w[:, :, OFF_KR - 64:OFF_KR]), [bw], [bwsw])
                fw.op(pool, lambda h: h.tensor_copy(wsw[:, :, 64:80], w[:, :, OFF_KR + 16:OFF_KR + 32]), [bw], [bwsw])
                fw.op(pool, lambda h: h.tensor_copy(wsw[:, :, 80:96], w[:, :, OFF_KR:OFF_KR + 16]), [bw], [bwsw])
                gmix = T(ph, [128, D], F32, "gmix"); gq = T(ph, [128, QL], F32, "gq"); gkv = T(ph, [128, KVL], F32, "gkv")
                bg = Buf()
                fw.dma(sp, gmix[:], g_mix[l].partition_broadcast(128), writes=[bg])
                fw.dma(sp, gq[:], g_q[l].partition_broadcast(128), writes=[bg])
                fw.dma(sp, gkv[:], g_kv[l].partition_broadcast(128), writes=[bg])
                xt = [T(ph, [128, 4, D], F32, "xt") for _ in range(2)]; bxt = [Buf(), Buf()]
                rc = [T(ph, [96, 2, 512], F32, "rc") for _ in range(2)]; brc = [Buf(), Buf()]
                hb = T(ph, [128, D], BF16, "hb"); bhb = Buf()
                junk = T(ph, [128, D], BF16, "junk"); bjunk = Buf()
                hTs = [T(ph, [128, 8, 512], BF16, "hT") for _ in range(2)]; bhTs = [Buf(), Buf()]
                ss = T(ph, [128, 4], F32, "ss"); bss = Buf()
                ss2 = T(ph, [128, 2], F32, "ss2"); bss2 = Buf()
                zs_t = T(ph, [128, 3, 512], BF16, "zs_t"); bzs = Buf()
                zf_t = T(ph, [128, 4, FW_], BF16, "zf_t"); bzf = Buf()
                cn = T(ph, [128, QL + KVL], BF16, "cn"); bcn = Buf()
                cT_t = T(ph, [128, 5, 512], BF16, "cT_t"); bcT = Buf()
                g_t = [T(ph, [128, 3 * D], BF16, "g_t") for _ in range(2)]; bgt = [Buf(), Buf()]
                t1 = T(ph, [96, 512], F32, "t1"); t2 = T(ph, [96, 512], F32, "t2"); bt1 = Buf(); bt2 = Buf()
                kp_t = T(ph, [96, 512], BF16, "kp_t"); bkp = Buf()
                pT = [P(ph, [128, 1024], BF16, "pT") for _ in range(2)]; bpT = [PB(), PB()]
                pa = [P(ph, [128, 512], F32, "pa") for _ in range(6)]; bpa = [PB() for _ in range(6)]
                pai = [0]; pti = [0]; evi = [0]

                def nb():
                    i = pai[0] % 6
                    pai[0] += 1
                    return pa[i], bpa[i]

                def nt():
                    i = pti[0] % 2
                    pti[0] += 1
                    return pT[i], bpT[i]

                def ev_eng():
                    evi[0] += 1
                    return act if evi[0] % 2 else dve

                def load_blk(b):
                    i = b % 2
                    fw.dma(sp, xt[i][:], xsrc[b * 512:(b + 1) * 512, :].rearrange("(j p) d -> p j d", p=128),
                           reads=[b_xsrc], writes=[bxt[i]])
                    fw.dma(sp, rc[i][64:96, :, :], c_rope[:, :, b * 512:(b + 1) * 512].rearrange("a d t -> d a t"),
                           writes=[brc[i]])

                def pre(b):
                    xi = xt[b % 2]; bxi = bxt[b % 2]
                    hT = hTs[b % 2]; bhT = bhTs[b % 2]
                    for j in range(4):
                        fw.op(act, lambda h: h.activation(junk[:], xi[:, j, :], AF.Square, accum_out=ss[:, j:j + 1]), [bxi], [bjunk, bss])
                    rstd_from_ss(None, ss, 4, D, bss, bss)
                    for j in range(4):
                        fw.op(dve, lambda h: h.scalar_tensor_tensor(hb[:], xi[:, j, :], ss[:, j:j + 1], gmix[:], ALU.mult, ALU.mult),
                              [bxi, bss, bg], [bhb])
                        pt, bpt = nt()
                        for c in range(8):
                            fw.op(pe, lambda h: h.transpose(pt[:, c * 128:(c + 1) * 128], hb[:, c * 128:(c + 1) * 128], identb[:]), [bhb, b_id], [bpt])
                        evac(ev_eng(), hT[:, :, j * 128:(j + 1) * 128], pt[:].rearrange("p (c t) -> p c t", c=8), [bpt], [bhT])

                def main(b):
                    rci = rc[b % 2]; brci = brc[b % 2]
                    hT = hTs[b % 2]; bhT = bhTs[b % 2]
                    for m in range(3):
                        ps, bps = nb()
                        for c in range(8):
                            fw.op(pe, lambda h: h.matmul(ps[:], w[:, c, OFF_S5 + m * 128:OFF_S5 + (m + 1) * 128], hT[:, c, :], start=(c == 0), stop=(c == 7)), [bw, bhT], [bps])
                        evac(ev_eng(), zs_t[:, m, :], ps[:], [bps], [bzs])
                    fw.dma(pool, zs5T_d[s][:, b * 512:(b + 1) * 512].rearrange("(m p) t -> p m t", p=128), zs_t[:], reads=[bzs], writes=[b_zs5T[s]])
                    ps, bps = nb()
                    ps2, bps2 = nb()
                    for c in range(8):
                        fw.op(pe, lambda h: h.matmul(ps[0:96, :], w[:, c, OFF_KR - 64:OFF_KR + 32], hT[:, c, :], start=(c == 0), stop=(c == 7)), [bw, bhT], [bps])
                    for c in range(8):
                        fw.op(pe, lambda h: h.matmul(ps2[0:96, :], wsw[:, c, :], hT[:, c, :], start=(c == 0), stop=(c == 7)), [bwsw, bhT], [bps2])
                    fw.op(dve, lambda h: h.tensor_tensor(t1[64:96, :], ps[64:96, :], rci[64:96, 0, :], ALU.mult), [bps, brci], [bt1])
                    fw.op(dve, lambda h: h.tensor_tensor(t2[64:96, :], ps2[64:96, :], rci[64:96, 1, :], ALU.mult), [bps2, brci], [bt2])
                    fw.op(pool, lambda h: h.tensor_tensor(kp_t[64:96, :], t1[64:96, :], t2[64:96, :], ALU.add), [bt1, bt2], [bkp])
                    fw.dma(pool, kpeT_d[s][:, b * 512:(b + 1) * 512], kp_t[64:96, :], reads=[bkp], writes=[b_kpeT[s]])
                    for j in range(4):
                        if j == 2:
                            yield
                        lhs = lambda c: hT[:, c, j * 128:(j + 1) * 128]
                        ps, bps = nb()
                        for c in range(8):
                            fw.op(pe, lambda h: h.matmul(ps[:, 0:FW_], lhs(c), w[:, c, OFF_FNET:OFF_FNET + FW_], start=(c == 0), stop=(c == 7)), [bw, bhT], [bps])
                        evac(ev_eng(), zf_t[:, j, :], ps[:, 0:FW_], [bps], [bzf])
                        psq, bpsq = nb()
                        for c in range(8):
                            fw.op(pe, lambda h: h.matmul(psq[:, 0:QL], lhs(c), w[:, c, OFF_Q:OFF_Q + QL], start=(c == 0), stop=(c == 7)), [bw, bhT], [bpsq])
                        psk, bpsk = nb()
                        for c in range(8):
                            fw.op(pe, lambda h: h.matmul(psk[:, 0:KVL], lhs(c), w[:, c, OFF_KV:OFF_KV + KVL], start=(c == 0), stop=(c == 7)), [bw, bhT], [bpsk])
                        fw.op(act, lambda h: h.activation(junk[:, 0:QL], psq[:, 0:QL], AF.Square, accum_out=ss2[:, 0:1]), [bpsq], [bjunk, bss2])
                        fw.op(act, lambda h: h.activation(junk[:, 0:KVL], psk[:, 0:KVL], AF.Square, accum_out=ss2[:, 1:2]), [bpsk], [bjunk, bss2])
                        fw.op(dve, lambda h: h.tensor_scalar(ss2[:, 0:1], ss2[:, 0:1], 1.0 / QL, EPS, ALU.mult, ALU.add), [bss2], [bss2])
                        fw.op(dve, lambda h: h.tensor_scalar(ss2[:, 1:2], ss2[:, 1:2], 1.0 / KVL, EPS, ALU.mult, ALU.add), [bss2], [bss2])
                        fw.op(act, lambda h: h.activation(ss2[:], ss2[:], AF.Sqrt), [bss2], [bss2])
                        fw.op(dve, lambda h: h.reciprocal(ss2[:], ss2[:]), [bss2], [bss2])
                        fw.op(dve, lambda h: h.scalar_tensor_tensor(cn[:, 0:QL], psq[:, 0:QL], ss2[:, 0:1], gq[:], ALU.mult, ALU.mult), [bpsq, bss2, bg], [bcn])
                        fw.op(dve, lambda h: h.scalar_tensor_tensor(cn[:, QL:QL + KVL], psk[:, 0:KVL], ss2[:, 1:2], gkv[:], ALU.mult, ALU.mult), [bpsk, bss2, bg], [bcn])
                        gt = g_t[j % 2]; bg_t = bgt[j % 2]
                        for q in range(6):
                            ps, bps = nb()
                            for c in range(8):
                                fw.op(pe, lambda h: h.matmul(ps[:], lhs(c), w[:, c, OFF_GATE + q * 512:OFF_GATE + (q + 1) * 512], start=(c == 0), stop=(c == 7)), [bw, bhT], [bps])
                            fw.op(act, lambda h: h.activation(gt[:, q * 512:(q + 1) * 512], ps[:], AF.Sigmoid), [bps], [bg_t])
                        r0 = b * 512 + j * 128
                        fw.dma(pool, gates_d[s][r0:r0 + 128, :], gt[:], reads=[bg_t], writes=[b_gates[s]])
                        pt, bpt = nt()
                        for c in range(5):
                            fw.op(pe, lambda h: h.transpose(pt[:, c * 128:(c + 1) * 128], cn[:, c * 128:(c + 1) * 128], identb[:]), [bcn, b_id], [bpt])
                        evac(ev_eng(), cT_t[:, :, j * 128:(j + 1) * 128], pt[:, 0:640].rearrange("p (c t) -> p c t", c=5), [bpt], [bcT])
                    fw.dma(pool, zf_d[s][b * 512:(b + 1) * 512, :].rearrange("(j p) f -> p j f", p=128), zf_t[:], reads=[bzf], writes=[b_zf[s]])
                    fw.dma(pool, cqnT_d[s][:, b * 512:(b + 1) * 512].rearrange("(m p) t -> p m t", p=128), cT_t[:, 0:3, :], reads=[bcT], writes=[b_cqnT[s]])
                    fw.dma(pool, ckvnT_d[s][:, b * 512:(b + 1) * 512].rearrange("(m p) t -> p m t", p=128), cT_t[:, 3:5, :], reads=[bcT], writes=[b_ckvnT[s]])
                for (s, xsrc, b_xsrc) in seqs:
                    load_blk(0)
                    pre(0)
                    for b in range(NB):
                        if b + 1 < NB:
                            load_blk(b + 1)
                        g = main(b)
                        next(g)
                        if b + 1 < NB:
                            pre(b + 1)
                        for _ in g:
                            pass
                fw.barrier()

        def phase_F(l, s):
            TCH = min(16, NTC)
            NHF = NTC // TCH
            with ExitStack() as ph:
                zf = T(ph, [128, NTC, FW_], BF16, "zf_sb"); bz = Buf()
                fw.dma(sp, zf[:], zf_d[s].rearrange("(c p) f -> p c f", p=128), reads=[b_zf[s]], writes=[bz])
                c64 = T(ph, [128, 2, 128], BF16, "c64"); bc64 = Buf()
                fw.dma(sp, c64[:], c_c64, writes=[bc64])
                dt_ = [T(ph, [128, TCH, 512], BF16, "dft") for _ in range(3)]; bdt = [Buf() for _ in range(3)]
                pq = T(ph, [128, 2, 3, 512], BF16, "pq"); bpq = Buf()
                fm_t = [T(ph, [128, 3, 512], BF16, "fm_t") for _ in range(2)]; bfm = [Buf(), Buf()]
                acc = [[P(ph, [128, 512], F32, "acc") for _ in range(3)] for _ in range(2)]
                bacc = [[PB() for _ in range(3)] for _ in range(2)]
                pf = P(ph, [128, 512], F32, "pf"); bpf = PB()
                tiles = [(kb, hf, cs) for kb in range(NB) for hf in range(NHF) for cs in range(2)]

                def load_tile(i):
                    kb, hf, cs = tiles[i]
                    src = c_dft[hf * TCH * 128:(hf + 1) * TCH * 128, cs, kb * 512:(kb + 1) * 512].rearrange("(c p) k -> p c k", p=128)
                    fw.dma(sp, dt_[i % 3][:], src, writes=[bdt[i % 3]])

                load_tile(0)
                if len(tiles) > 1:
                    load_tile(1)
                for i, (kb, hf, cs) in enumerate(tiles):
                    if i + 2 < len(tiles):
                        load_tile(i + 2)
                    dti = dt_[i % 3]; bdti = bdt[i % 3]
                    for tc in range(TCH):
                        for m in range(3):
                            fw.op(pe, lambda h: h.matmul(acc[cs][m][:], zf[:, hf * TCH + tc, m * 128:(m + 1) * 128], dti[:, tc, :],
                                                         start=(hf == 0 and tc == 0), stop=(hf == NHF - 1 and tc == TCH - 1)), [bz, bdti], [bacc[cs][m]])
                    if hf == NHF - 1 and cs == 1:
                        k = 0
                        for c2 in range(2):
                            for m in range(3):
                                evac(act if k % 2 else dve, pq[:, c2, m, :], acc[c2][m][:], [bacc[c2][m]], [bpq])
                                k += 1
                        fmi = fm_t[kb % 2]; bfmi = bfm[kb % 2]
                        for m in range(3):
                            fw.op(pe, lambda h: h.matmul(pf[:], c64[:, 0, :], pq[:, 0, m, :], start=True, stop=False), [bc64, bpq], [bpf])
                            fw.op(pe, lambda h: h.matmul(pf[:], c64[:, 1, :], pq[:, 1, m, :], start=False, stop=True), [bc64, bpq], [bpf])
                            evac(act if m % 2 else dve, fmi[:, m, :], pf[:], [bpf], [bfmi])
                        fw.dma(pool, fmT_d[s][:, kb * 512:(kb + 1) * 512].rearrange("(m p) t -> p m t", p=128), fmi[:], reads=[bfmi], writes=[b_fmT[s]])
                fw.barrier()

        def sincos(ph, ang, n, want_cos, bang, name):
            TWO_PI = 2.0 * math.pi
            kf = T(ph, [128, n], F32, name + "kf"); ki = T(ph, [128, n], I32, name + "ki")
            r = T(ph, [128, n], F32, name + "r"); m = T(ph, [128, n], F32, name + "m")
            out = T(ph, [128, n], F32, name + "o")
            bb = Buf()
            shift = (math.pi / 2.0) if want_cos else 0.0
            fw.op(dve, lambda h: h.tensor_scalar(r[:], ang[:], shift, None, ALU.add), [bang], [bb])
            fw.op(dve, lambda h: h.tensor_scalar(kf[:], r[:], 1.0 / TWO_PI, None, ALU.mult), [bb], [bb])
            fw.op(dve, lambda h: h.tensor_copy(ki[:], kf[:]), [bb], [bb])
            fw.op(dve, lambda h: h.tensor_copy(kf[:], ki[:]), [bb], [bb])
            fw.op(dve, lambda h: h.scalar_tensor_tensor(r[:], kf[:], -TWO_PI, r[:], ALU.mult, ALU.add), [bb], [bb])
            fw.op(dve, lambda h: h.tensor_scalar(m[:], r[:], math.pi, TWO_PI, ALU.is_gt, ALU.mult), [bb], [bb])
            fw.op(dve, lambda h: h.tensor_tensor(r[:], r[:], m[:], ALU.subtract), [bb], [bb])
            fw.op(dve, lambda h: h.tensor_scalar(m[:], r[:], -math.pi, TWO_PI, ALU.is_lt, ALU.mult), [bb], [bb])
            fw.op(dve, lambda h: h.tensor_tensor(r[:], r[:], m[:], ALU.add), [bb], [bb])
            fw.op(act, lambda h: h.activation(out[:], r[:], AF.Sin), [bb], [bb])
            return out, bb

        def phase_S(l):
            with ExitStack() as ph:
                bp = Buf()
                LR = T(ph, [128, 48], F32, "LR"); LI = T(ph, [128, 48], F32, "LI"); DT = T(ph, [128, 48], F32, "DT")
                NAT = T(ph, [48, 2, 128], F32, "NAT"); bnat = Buf()
                for half in range(2):
                    fw.dma(sp, NAT[:, 0, half * 64:(half + 1) * 64], lam_re[l].rearrange("d g n -> (d g) n"), writes=[bnat])
                    fw.dma(sp, NAT[:, 1, half * 64:(half + 1) * 64], lam_im[l].rearrange("d g n -> (d g) n"), writes=[bnat])
                with nc.psum_tensor(f"pnat_{l}", [128, 512], F32) as pnat:
                    bpn = Buf()
                    fw.op(pe, lambda h: h.transpose(pnat[:, 0:48], NAT[:, 0, :], identf[0:48, 0:48]), [bnat, b_id], [bpn])
                    fw.op(pe, lambda h: h.transpose(pnat[:, 64:112], NAT[:, 1, :], identf[0:48, 0:48]), [bnat, b_id], [bpn])
                    fw.op(dve, lambda h: h.tensor_copy(LR[:], pnat[:, 0:48]), [bpn], [bp])
                    fw.op(dve, lambda h: h.tensor_copy(LI[:], pnat[:, 64:112]), [bpn], [bp])
                    fw.barrier()
                fw.dma(sp, DT[:], log_dt[l].rearrange("d g -> (d g)").partition_broadcast(128), writes=[bp])
                fw.op(act, lambda h: h.activation(DT[:], DT[:], AF.Exp), [bp], [bp])
                lr = T(ph, [128, 48], F32, "lr"); li = T(ph, [128, 48], F32, "li")
                fw.op(dve, lambda h: h.tensor_tensor(lr[:], LR[:], DT[:], ALU.mult), [bp], [bp])
                fw.op(dve, lambda h: h.tensor_tensor(li[:], LI[:], DT[:], ALU.mult), [bp], [bp])
                li64 = T(ph, [128, 48], F32, "li64")
                fw.op(dve, lambda h: h.tensor_scalar(li64[:], li[:], float(TST), None, ALU.mult), [bp], [bp])
                sn, b1 = sincos(ph, li, 48, False, bp, "s1")
                cs_, b2 = sincos(ph, li, 48, True, bp, "c1")
                sn64, b3 = sincos(ph, li64, 48, False, bp, "s2")
                cs64, b4 = sincos(ph, li64, 48, True, bp, "c2")
                mag = T(ph, [128, 48], F32, "mag"); mag64 = T(ph, [128, 48], F32, "mag64")
                fw.op(act, lambda h: h.activation(mag[:], lr[:], AF.Exp), [bp], [bp])
                fw.op(act, lambda h: h.activation(mag64[:], lr[:], AF.Exp, scale=float(TST)), [bp], [bp])
                v1 = T(ph, [128, 48], F32, "v1"); v2 = T(ph, [128, 48], F32, "v2"); lbi = T(ph, [128, 48], F32, "lbi")
                w1 = T(ph, [128, 48], F32, "w1"); w2 = T(ph, [128, 48], F32, "w2")
                fw.op(dve, lambda h: h.tensor_tensor(v1[:], mag[:], cs_[:], ALU.mult), [bp, b2], [bp])
                fw.op(dve, lambda h: h.tensor_tensor(lbi[:], mag[:], sn[:], ALU.mult), [bp, b1], [bp])
                fw.op(dve, lambda h: h.tensor_scalar(v2[:], lbi[:], misc[:, 0:1], None, ALU.mult), [bp, b_id], [bp])
                fw.op(dve, lambda h: h.tensor_tensor(w1[:], mag64[:], cs64[:], ALU.mult), [bp, b4], [bp])
                fw.op(dve, lambda h: h.tensor_tensor(w2[:], mag64[:], sn64[:], ALU.mult), [bp, b3], [bp])
                fw.op(dve, lambda h: h.tensor_scalar(w2[:], w2[:], misc[:, 0:1], None, ALU.mult), [bp, b_id], [bp])
                den = T(ph, [128, 48], F32, "den"); tmp = T(ph, [128, 48], F32, "tmp"); lm1 = T(ph, [128, 48], F32, "lm1")
                cr = T(ph, [128, 48], F32, "cr"); ci = T(ph, [128, 48], F32, "ci")
                fw.op(dve, lambda h: h.tensor_tensor(den[:], LR[:], LR[:], ALU.mult), [bp], [bp])
                fw.op(dve, lambda h: h.tensor_tensor(tmp[:], LI[:], LI[:], ALU.mult), [bp], [bp])
                fw.op(dve, lambda h: h.tensor_tensor(den[:], den[:], tmp[:], ALU.add), [bp], [bp])
                fw.op(dve, lambda h: h.reciprocal(den[:], den[:]), [bp], [bp])
                fw.op(dve, lambda h: h.tensor_scalar(lm1[:], v1[:], -1.0, None, ALU.add), [bp], [bp])
                fw.op(dve, lambda h: h.tensor_tensor(cr[:], lm1[:], LR[:], ALU.mult), [bp], [bp])
                fw.op(dve, lambda h: h.tensor_tensor(tmp[:], lbi[:], LI[:], ALU.mult), [bp], [bp])
                fw.op(dve, lambda h: h.tensor_tensor(cr[:], cr[:], tmp[:], ALU.add), [bp], [bp])
                fw.op(dve, lambda h: h.tensor_tensor(cr[:], cr[:], den[:], ALU.mult), [bp], [bp])
                fw.op(dve, lambda h: h.tensor_tensor(ci[:], lbi[:], LR[:], ALU.mult), [bp], [bp])
                fw.op(dve, lambda h: h.tensor_tensor(tmp[:], lm1[:], LI[:], ALU.mult), [bp], [bp])
                fw.op(dve, lambda h: h.tensor_tensor(ci[:], ci[:], tmp[:], ALU.subtract), [bp], [bp])
                fw.op(dve, lambda h: h.tensor_tensor(ci[:], ci[:], den[:], ALU.mult), [bp], [bp])
                fw.op(dve, lambda h: h.tensor_scalar(ci[:], ci[:], misc[:, 1:2], None, ALU.mult), [bp, b_id], [bp])
                X1 = T(ph, [128, 48, 16], F32, "X1"); X2 = T(ph, [128, 48, 16], F32, "X2")
                fw.dma(sp, X1[0:64, :, :], b_re[l].rearrange("d g n q -> n (d g) q"), writes=[bp])
                fw.dma(sp, X1[64:128, :, :], b_im[l].rearrange("d g n q -> n (d g) q"), writes=[bp])
                fw.dma(sp, X2[0:64, :, :], b_im[l].rearrange("d g n q -> n (d g) q"), writes=[bp])
                fw.dma(sp, X2[64:128, :, :], b_re[l].rearrange("d g n q -> n (d g) q"), writes=[bp])
                fw.op(dve, lambda h: h.tensor_tensor(X1[:], X1[:], cr[:].unsqueeze(2).to_broadcast([128, 48, 16]), ALU.mult), [bp], [bp])
                fw.op(dve, lambda h: h.tensor_tensor(X2[:], X2[:], ci[:].unsqueeze(2).to_broadcast([128, 48, 16]), ALU.mult), [bp], [bp])
                fw.op(dve, lambda h: h.tensor_tensor(X1[:], X1[:], X2[:], ALU.add), [bp], [bp])
                CN = T(ph, [128, 6, 2, 64], F32, "CN")
                fw.dma(sp, CN[:, :, 0, :], c_re[l].rearrange("d (j e) p n -> (e p) (d j) n", e=8), writes=[bp])
                fw.dma(sp, CN[:, :, 1, :], c_im[l].rearrange("d (j e) p n -> (e p) (d j) n", e=8), writes=[bp])
                fw.op(dve, lambda h: h.tensor_scalar(CN[:, :, 1, :], CN[:, :, 1, :], -1.0, None, ALU.mult), [bp], [bp])
                dcol = T(ph, [128, 3], F32, "dcol")
                for j in range(3):
                    fw.dma(sp, dcol[:, j:j + 1], s5_d[l, j * 128:(j + 1) * 128].rearrange("(p o) -> p o", o=1), writes=[bp])
                wg = T(ph, [128, 3, 2 * S5W], BF16, "wg"); bwg = Buf()
                for c in range(3):
                    load_w(wg[:, c, :], w_glu[l, c * 128:(c + 1) * 128, :], bwg, 2 * S5W)

                if S_STOP == 1:
                    fw.barrier()
                    return
                GW = min(1024, L)
                ge1 = T(ph, [128, GW], F32, "ge1"); ge2 = T(ph, [128, GW], F32, "ge2"); bge1 = Buf(); bge2 = Buf()
                ygo = [T(ph, [128, GW], BF16, "ygo") for _ in range(2)]; bygo = [Buf(), Buf()]
                dummy_w = T(ph, [128, 32], BF16, "dummy_w"); bdw = Buf()
                fw.op(pool, lambda h: h.memset(dummy_w[:], 0.0), [], [bdw])
                A = T(ph, [128, 16, 128], F32, "A"); A64 = T(ph, [128, 16, 128], F32, "A64")
                Bm = T(ph, [128, 16, 128], BF16, "Bm"); Cm = T(ph, [128, 16, 128], F32, "Cm")
                bA = Buf(); bA64 = Buf(); bBm = Buf(); bCm = Buf()
                U = T(ph, [128, NS, L], BF16, "U"); bU = Buf()
                Y = T(ph, [128, NS, L], F32, "Y"); bY = Buf()
                H = T(ph, [128, 16, NC2], F32, "H"); bH = [Buf() for _ in range(4)]
                IA = T(ph, [128, 16, NC2], F32, "IA"); bIA = [Buf() for _ in range(4)]
                pr = [P(ph, [128, 512], F32, "pr") for _ in range(7)]
                bpr = [PB() for _ in range(7)]
                ptr = pr[6]; bptr = bpr[6]
                evi = [0]
                GB = 4
                assert GB * NC2 <= 512

                def ev_eng():
                    evi[0] += 1
                    return act if evi[0] % 2 else dve

                for j in range(3):
                    for d in range(2):
                        for e in range(8):
                            gd = d * 8 + e
                            col = d * 24 + j * 8 + e
                            fw.op(dve, lambda h: h.tensor_scalar(A[:, gd, :], identf[:], v1[:, col:col + 1], None, ALU.mult), [bp, b_id], [bA])
                            fw.op(dve, lambda h: h.scalar_tensor_tensor(A[:, gd, :], jswap[:], v2[:, col:col + 1], A[:, gd, :], ALU.mult, ALU.add), [bp, b_id, bA], [bA])
                            fw.op(dve, lambda h: h.tensor_scalar(A64[:, gd, :], identf[:], w1[:, col:col + 1], None, ALU.mult), [bp, b_id], [bA64])
                            fw.op(dve, lambda h: h.scalar_tensor_tensor(A64[:, gd, :], jswap[:], w2[:, col:col + 1], A64[:, gd, :], ALU.mult, ALU.add), [bp, b_id, bA64], [bA64])
                        c0 = d * 24 + j * 8
                        fw.op(pe, lambda h: h.transpose(ptr[:, 0:128], X1[:, c0:c0 + 8, :].rearrange("p g q -> p (g q)"), identf[:]), [bp, b_id], [bptr])
                        for e in range(8):
                            gd = d * 8 + e
                            fw.op(dve, lambda h: h.tensor_scalar(Bm[:, gd, :], ptr[:, 0:128], misc[:, 8 + e:9 + e], None, ALU.mult), [bptr, b_id], [bBm])
                        fw.op(pe, lambda h: h.transpose(ptr[:, 128:256], CN[:, d * 3 + j, :, :].rearrange("p a n -> p (a n)"), identf[:]), [bp, b_id], [bptr])
                        fw.op(pool, lambda h: h.memset(Cm[:, d * 8:(d + 1) * 8, :], 0.0), [], [bCm])
                        for e in range(8):
                            gd = d * 8 + e
                            fw.op(dve, lambda h: h.tensor_copy(Cm[:, gd, e * 16:(e + 1) * 16], ptr[:, 128 + e * 16:128 + (e + 1) * 16]), [bptr], [bCm])
                    if S_STOP == 2:
                        fw.barrier()
                        return
                    for s in range(NS):
                        fw.dma(sp, U[:, s, :], zs5T_d[s][j * 128:(j + 1) * 128, :], reads=[b_zs5T[s]], writes=[bU])
                    fw.op(dve, lambda h: h.tensor_scalar(Y[:], U[:], dcol[:, j:j + 1], None, ALU.mult), [bU, bp], [bY])

                    def ucols(d, tau):
                        pos = tau if d == 0 else TST - 1 - tau
                        return U[:, :, pos:L:TST]

                    def step(tau, X, bX, first, with_y):
                        for q in range(4):
                            bank, bb = pr[q], bpr[q]
                            d = q // 2
                            for i in range(GB):
                                gd = q * GB + i
                                reg = bank[:, i * NC2:(i + 1) * NC2]
                                fw.op(pe, lambda h: h.matmul(reg, Bm[:, gd, :], ucols(d, tau), start=(i == 0), stop=first, skip_group_check=True), [bBm, bU], [bb])
                            if not first:
                                for i in range(GB):
                                    gd = q * GB + i
                                    reg = bank[:, i * NC2:(i + 1) * NC2]
                                    fw.op(pe, lambda h: h.matmul(reg, A[:, gd, :], X[:, gd, :], start=False, stop=True, skip_group_check=True), [bA, bX[q]], [bb])
                            evac(ev_eng(), X[:, q * GB:(q + 1) * GB, :], bank[:, 0:GB * NC2].rearrange("p (g c) -> p g c", g=GB), [bb], [bX[q]])
                            if with_y and q % 2 == 1:
                                yb_, byb_ = pr[4 + d], bpr[4 + d]
                                for e in range(8):
                                    gd = d * 8 + e
                                    fw.op(pe, lambda h: h.matmul(yb_[:, 0:NC2], Cm[:, gd, :], X[:, gd, :], start=(e == 0), stop=(e == 7)), [bCm, bX[gd // GB]], [byb_])
                                pos = tau if d == 0 else TST - 1 - tau
                                yv = Y[:, :, pos:L:TST]
                                fw.op(dve, lambda h: h.tensor_tensor(yv, yv, yb_[:, 0:NC2].rearrange("p (s c) -> p s c", s=NS), ALU.add), [byb_, bY], [bY])

                    for tau in range(TST):
                        step(tau, H, bH, tau == 0, False)
                    if S_STOP == 3:
                        fw.barrier()
                        return
                    Hv = H[:].rearrange("p g (s c) -> p g s c", s=NS)
                    IAv = IA[:].rearrange("p g (s c) -> p g s c", s=NS)
                    for q in range(4):
                        d = q // 2
                        c_first = 0 if d == 0 else NC_ - 1
                        fw.op(pool, lambda h: h.memset(IAv[:, q * GB:(q + 1) * GB, :, c_first:c_first + 1], 0.0), [], [bIA[q]])
                    for cstep in range(NC_ - 1):
                        for q in range(4):
                            d = q // 2
                            c = cstep if d == 0 else NC_ - 1 - cstep
                            cn_ = c + 1 if d == 0 else c - 1
                            bank, bb = pr[q], bpr[q]
                            for i in range(GB):
                                gd = q * GB + i
                                fw.op(pe, lambda h: h.matmul(bank[:, i * NS:(i + 1) * NS], A64[:, gd, :], IAv[:, gd, :, c], start=(i == 0), stop=True, skip_group_check=True), [bA64, bIA[q]], [bb])
                            fw.op(dve, lambda h: h.tensor_tensor(IAv[:, q * GB:(q + 1) * GB, :, cn_], bank[:, 0:GB * NS].rearrange("p (g s) -> p g s", g=GB),
                                                                 Hv[:, q * GB:(q + 1) * GB, :, c], ALU.add), [bb, bH[q]], [bIA[q]])
                    if S_STOP == 4:
                        fw.barrier()
                        return
                    for tau in range(TST):
                        step(tau, IA, bIA, False, True)
                    if S_STOP == 5:
                        fw.barrier()
                        return
                    kg = 0
                    for s in range(NS):
                        for hb_ in range(L // GW):
                            wd = GW
                            sl = slice(hb_ * wd, (hb_ + 1) * wd)
                            yo = ygo[kg % 2]; byo = bygo[kg % 2]
                            kg += 1
                            g1 = ge1[:, 0:wd]; g2 = ge2[:, 0:wd]
                            yv = Y[:, s, sl]
                            fw.op(dve, lambda h: h.tensor_tensor(g1, yv, yv, ALU.mult), [bY], [bge1])
                            fw.op(dve, lambda h: h.tensor_scalar(g1, g1, 0.044715, 1.0, ALU.mult, ALU.add), [bge1], [bge1])
                            fw.op(pool, lambda h: h.tensor_tensor(g1, g1, yv, ALU.mult), [bge1, bY], [bge1])
                            fw.op(act, lambda h: h.activation(g2, g1, AF.Sigmoid, scale=1.5957691216057308), [bge1], [bge2])
                            fw.op(dve, lambda h: h.tensor_tensor(yo[:], yv, g2, ALU.mult), [bY, bge2], [byo])
                            fw.dma(pool, ygT_d[s][j * 128:(j + 1) * 128, sl], yo[:], reads=[byo], writes=[b_ygT[s]])
                fw.barrier()
                fw.op(pe, lambda h: h.matmul(pr[5][0:32, 0:32], dummy_w[:], dummy_w[:], start=True, stop=True), [bdw], [bptr])
                fw.barrier()
                gl_t = [T(ph, [128, 3, 512], BF16, "gl_t") for _ in range(2)]; bgl = [Buf(), Buf()]
                yin = [T(ph, [128, 3, 512], BF16, "yin") for _ in range(2)]; byin = [Buf(), Buf()]
                sg = T(ph, [128, 512], F32, "sg"); bsg = Buf()
                pg = [(pr[i], bpr[i]) for i in range(4)]
                k = 0
                for s in range(NS):
                    for tb in range(NB):
                        gt = gl_t[k % 2]; bgt = bgl[k % 2]
                        yi_ = yin[k % 2]; byi_ = byin[k % 2]
                        k += 1
                        fw.dma(sp, yi_[:], ygT_d[s][:, tb * 512:(tb + 1) * 512].rearrange("(m p) t -> p m t", p=128), reads=[b_ygT[s]], writes=[byi_])
                        for m in range(3):
                            pa_, bpa_ = pg[(2 * m) % 4]
                            pg_, bpg_ = pg[(2 * m + 1) % 4]
                            for c in range(3):
                                fw.op(pe, lambda h: h.matmul(pa_[:], wg[:, c, m * 128:(m + 1) * 128], yi_[:, c, :], start=(c == 0), stop=(c == 2)), [bwg, byi_], [bpa_])
                            for c in range(3):
                                fw.op(pe, lambda h: h.matmul(pg_[:], wg[:, c, S5W + m * 128:S5W + (m + 1) * 128], yi_[:, c, :], start=(c == 0), stop=(c == 2)), [bwg, byi_], [bpg_])
                            fw.op(act, lambda h: h.activation(sg[:], pg_[:], AF.Sigmoid), [bpg_], [bsg])
                            fw.op(dve, lambda h: h.tensor_tensor(gt[:, m, :], pa_[:], sg[:], ALU.mult), [bpa_, bsg], [bgt])
                        fw.dma(pool, gluT_d[s][:, tb * 512:(tb + 1) * 512].rearrange("(m p) t -> p m t", p=128), gt[:], reads=[bgt], writes=[b_gluT[s]])
                fw.barrier()

        def phase_M(l, s):
            with ExitStack() as ph:
                cq = T(ph, [128, 3, L], BF16, "cq"); ckv = T(ph, [128, 2, L], BF16, "ckv"); bc = Buf()
                fw.dma(sp, cq[:], cqnT_d[s].rearrange("(m p) t -> p m t", p=128), reads=[b_cqnT[s]], writes=[bc])
                fw.dma(sp, ckv[:], ckvnT_d[s].rearrange("(m p) t -> p m t", p=128), reads=[b_ckvnT[s]], writes=[bc])
                kpe = T(ph, [96, L], BF16, "kpe")
                fw.dma(sp, kpe[64:96, :], kpeT_d[s], reads=[b_kpeT[s]], writes=[bc])
                rct = T(ph, [96, 2, L], F32, "rct")
                fw.dma(sp, rct[64:96, :, :], c_rope.rearrange("a d t -> d a t"), writes=[bc])
                wq = T(ph, [128, 3, NH * 96], BF16, "wq"); wqs = T(ph, [128, 3, NH * 96], BF16, "wqs"); wkv = T(ph, [128, 2, NH * 128], BF16, "wkv")
                bwq = Buf(); bwk = Buf()
                for c in range(3):
                    load_w(wq[:, c, :], w_qb[l, c * 128:(c + 1) * 128, :], bwq, NH * 96)
                for c in range(2):
                    load_w(wkv[:, c, :], w_kvb[l, c * 128:(c + 1) * 128, :], bwk, NH * 128)
                for c in range(3):
                    wq4 = wq[:, c, :].rearrange("p (h e) -> p h e", h=NH)
                    wqs4 = wqs[:, c, :].rearrange("p (h e) -> p h e", h=NH)
                    fw.op(pool, lambda h: h.tensor_copy(wqs4[:, :, 0:64], wq4[:, :, 0:64]), [bwq], [bwq])
                    fw.op(pool, lambda h: h.tensor_copy(wqs4[:, :, 64:80], wq4[:, :, 80:96]), [bwq], [bwq])
                    fw.op(pool, lambda h: h.tensor_copy(wqs4[:, :, 80:96], wq4[:, :, 64:80]), [bwq], [bwq])
                QT = [T(ph, [96, L], BF16, "QT") for _ in range(2)]; bQT = [Buf(), Buf()]
                KT = [T(ph, [96, L], BF16, "KT") for _ in range(2)]; bKT = [Buf(), Buf()]
                V = [T(ph, [128, NTC, 65], BF16, "V") for _ in range(2)]; bV = [Buf(), Buf()]
                for i in range(2):
                    fw.op(pool, lambda h: h.memset(V[i][:, :, 64:65], 1.0), [], [bV[i]])
                t1 = T(ph, [96, 512], F32, "t1"); t2 = T(ph, [96, 512], F32, "t2"); bt1 = Buf(); bt2 = Buf()
                KP = 2 if NTC % 2 == 0 else 1
                pT_ = [T(ph, [128, KP, 512], BF16, "pT") for _ in range(3)]; bpT_ = [Buf() for _ in range(3)]
                o_t = [T(ph, [128, 4, 64], BF16, "o_t") for _ in range(2)]; bo_t = [Buf(), Buf()]
                oTs = [T(ph, [65, 512], F32, "oTs") for _ in range(2)]; boTs = [Buf(), Buf()]
                rec = T(ph, [128, 4], F32, "rec"); brec = Buf()
                psS = [P(ph, [128, KP, 512], F32, "psS") for _ in range(2)]; bpsS = [PB() for _ in range(2)]
                psO = [P(ph, [128, 512], F32, "psO") for _ in range(2)]; bpsO = [PB(), PB()]
                psP = [P(ph, [128, 512], F32, "psP") for _ in range(2)]; bpsP = [PB() for _ in range(2)]
                ppi = [0]; si = [0]; oi = [0]; evi = [0]

                def npp():
                    i = ppi[0] % 2
                    ppi[0] += 1
                    return psP[i], bpsP[i]

                def ev_eng():
                    return dve

                def prep_head(hh):
                    i = hh % 2
                    qt, bqt, kt, bkt, v, bv = QT[i], bQT[i], KT[i], bKT[i], V[i], bV[i]
                    for tb in range(NB):
                        sl = slice(tb * 512, (tb + 1) * 512)
                        ps, bps = npp()
                        ps2, bps2 = npp()
                        for c in range(3):
                            fw.op(pe, lambda h: h.matmul(ps[0:96, :], wq[:, c, hh * 96:(hh + 1) * 96], cq[:, c, sl], start=(c == 0), stop=(c == 2)), [bwq, bc], [bps])
                        for c in range(3):
                            fw.op(pe, lambda h: h.matmul(ps2[0:96, :], wqs[:, c, hh * 96:(hh + 1) * 96], cq[:, c, sl], start=(c == 0), stop=(c == 2)), [bwq, bc], [bps2])
                        fw.op(dve, lambda h: h.tensor_scalar(qt[0:64, sl], ps[0:64, :], SCALE, None, ALU.mult), [bps], [bqt])
                        fw.op(dve, lambda h: h.scalar_tensor_tensor(t1[64:96, :], ps[64:96, :], SCALE, rct[64:96, 0, sl], ALU.mult, ALU.mult), [bps, bc], [bt1])
                        fw.op(dve, lambda h: h.scalar_tensor_tensor(t2[64:96, :], ps2[64:96, :], SCALE, rct[64:96, 1, sl], ALU.mult, ALU.mult), [bps2, bc], [bt2])
                        fw.op(pool, lambda h: h.tensor_tensor(qt[64:96, sl], t1[64:96, :], t2[64:96, :], ALU.add), [bt1, bt2], [bqt])
                        ps3, bps3 = npp()
                        for c in range(2):
                            fw.op(pe, lambda h: h.matmul(ps3[0:64, :], wkv[:, c, hh * 128:hh * 128 + 64], ckv[:, c, sl], start=(c == 0), stop=(c == 1)), [bwk, bc], [bps3])
                        evac(ev_eng(), kt[0:64, sl], ps3[0:64, :], [bps3], [bkt])
                        yield
                    fw.op(pool, lambda h: h.tensor_copy(kt[64:96, :], kpe[64:96, :]), [bc], [bkt])
                    for t8 in range(NTC // 8 if NTC >= 8 else 1):
                        nt8 = min(8, NTC)
                        ps, bps = npp()
                        for tt in range(nt8):
                            tcx = t8 * 8 + tt
                            for c in range(2):
                                fw.op(pe, lambda h: h.matmul(ps[:, tt * 64:(tt + 1) * 64], ckv[:, c, tcx * 128:(tcx + 1) * 128], wkv[:, c, hh * 128 + 64:hh * 128 + 128], start=(c == 0 and tt == 0), stop=(c == 1), skip_group_check=True), [bwk, bc], [bps])
                        evac(ev_eng(), v[:, t8 * 8:t8 * 8 + nt8, 0:64], ps[:, 0:nt8 * 64].rearrange("p (t e) -> p t e", e=64), [bps], [bv])
                        yield

                NK = NTC // KP
                for _ in prep_head(0):
                    pass
                for hh in range(NH):
                    prep = prep_head(hh + 1) if hh + 1 < NH else iter(())
                    i = hh % 2
                    qt, bqt, kt, bkt, v, bv = QT[i], bQT[i], KT[i], bKT[i], V[i], bV[i]
                    its = [(qb, kc2) for qb in range(NB) for kc2 in range(NK)]

                    def emit_S(n):
                        qb, kc2 = its[n]
                        pss, bpss = psS[n % 2], bpsS[n % 2]
                        for u in range(KP):
                            kc = kc2 * KP + u
                            fw.op(pe, lambda h: h.matmul(pss[:, u, :], kt[:, kc * 128:(kc + 1) * 128], qt[:, qb * 512:(qb + 1) * 512], start=True, stop=True), [bkt, bqt], [bpss])

                    def epilogue_rest(qb, ots, bots, ot, bot):
                        ptr_, bptr_ = npp()
                        for qs in range(4):
                            fw.op(pe, lambda h: h.transpose(ptr_[:, qs * 65:(qs + 1) * 65], ots[:, qs * 128:(qs + 1) * 128], identf[0:65, 0:65]), [bots, b_id], [bptr_])
                        pv_ = ptr_[:, 0:260].rearrange("p (q e) -> p q e", e=65)
                        fw.op(dve, lambda h: h.reciprocal(rec[:], pv_[:, :, 64]), [bptr_], [brec])
                        fw.op(dve, lambda h: h.tensor_tensor(ot[:], pv_[:, :, 0:64], rec[:].unsqueeze(2).to_broadcast([128, 4, 64]), ALU.mult), [bptr_, brec], [bot])
                        fw.dma(pool, o_d[s][qb * 512:(qb + 1) * 512, hh * 64:(hh + 1) * 64].rearrange("(q p) e -> p q e", p=128), ot[:], reads=[bot], writes=[b_o[s]])

                    pending = []
                    emit_S(0)
                    pstep = max(1, len(its) // 14)
                    for n, (qb, kc2) in enumerate(its):
                        if n + 1 < len(its):
                            emit_S(n + 1)
                        if n % pstep == pstep - 1:
                            next(prep, None)
                        po, bpo = psO[(oi[0] + qb) % 2], bpsO[(oi[0] + qb) % 2]
                        pss, bpss = psS[n % 2], bpsS[n % 2]
                        pt, bpt = pT_[n % 3], bpT_[n % 3]
                        fw.op(act, lambda h: h.activation(pt[:], pss[:], AF.Exp), [bpss], [bpt])
                        if pending and pending[0][0] <= n:
                            epilogue_rest(*pending.pop(0)[1])
                        for u in range(KP):
                            kc = kc2 * KP + u
                            fw.op(pe, lambda h: h.matmul(po[0:65, :], v[:, kc, :], pt[:, u, :], start=(kc == 0), stop=(kc == NTC - 1)), [bpt, bv], [bpo])
                        if kc2 == NK - 1:
                            k_ = (oi[0] + qb) % 2
                            ots, bots = oTs[k_], boTs[k_]
                            ot, bot = o_t[k_], bo_t[k_]
                            fw.op(dve, lambda h: h.tensor_copy(ots[:], po[0:65, :]), [bpo], [bots])
                            pending.append((n + 2, (qb, ots, bots, ot, bot)))
                    while pending:
                        epilogue_rest(*pending.pop(0)[1])
                    for _ in prep:
                        pass
                    oi[0] += NB
                fw.barrier()

        def phase_C1(l, seqs, rider=iter(())):
            with ExitStack() as ph:
                wf = T(ph, [128, 3, D], BF16, "wf"); ws = T(ph, [128, 3, D], BF16, "ws")
                wo = T(ph, [128, 8, D], BF16, "wo"); wout = T(ph, [128, 8, D], BF16, "wout")
                bw = Buf()
                for c in range(3):
                    load_w(wf[:, c, :], w_fnet[l, c * 128:(c + 1) * 128, :], bw, D)
                    load_w(ws[:, c, :], w_s5[l, c * 128:(c + 1) * 128, :], bw, D)
                for c in range(8):
                    load_w(wo[:, c, :], w_o[l, c * 128:(c + 1) * 128, :], bw, D)
                    load_w(wout[:, c, :], w_out[l, c * 128:(c + 1) * 128, :], bw, D)
                NBUF = 3
                fm = [T(ph, [128, 3, 128], BF16, "fm") for _ in range(NBUF)]
                gl = [T(ph, [128, 3, 128], BF16, "gl") for _ in range(NBUF)]
                ot = [T(ph, [128, D], BF16, "ot") for _ in range(NBUF)]
                gt = [T(ph, [128, 3 * D], BF16, "gt") for _ in range(NBUF)]
                xt = [T(ph, [128, D], F32, "xt") for _ in range(NBUF)]
                bin_ = [Buf() for _ in range(NBUF)]
                oT = T(ph, [128, 8, 128], BF16, "oT"); boT = Buf()
                m1 = [T(ph, [128, D], F32, "m1") for _ in range(2)]; m2 = [T(ph, [128, D], F32, "m2") for _ in range(2)]
                m3 = [T(ph, [128, D], F32, "m3") for _ in range(2)]
                bm1 = [Buf(), Buf()]; bm2 = [Buf(), Buf()]; bm3 = [Buf(), Buf()]
                mb = [T(ph, [128, D], BF16, "mb") for _ in range(2)]; bmb = [Buf(), Buf()]
                mT = T(ph, [128, 8, 128], BF16, "mT"); bmT = Buf()
                xo = [T(ph, [128, D], F32, "xo") for _ in range(2)]; bxo = [Buf(), Buf()]
                pT = [P(ph, [128, 1024], BF16, "pT") for _ in range(2)]; bpT = [PB(), PB()]
                pa = [P(ph, [128, 512], F32, "pa") for _ in range(6)]; bpa = [PB() for _ in range(6)]
                pai = [0]; pti = [0]

                def nb():
                    i = pai[0] % 6
                    pai[0] += 1
                    return pa[i], bpa[i]

                def nt():
                    i = pti[0] % 2
                    pti[0] += 1
                    return pT[i], bpT[i]

                def load(i):
                    k = i % NBUF
                    sl = slice(i * 128, (i + 1) * 128)
                    fw.dma(sp, fm[k][:], fmT_d[s][:, sl].rearrange("(m p) t -> p m t", p=128), reads=[b_fmT[s]], writes=[bin_[k]])
                    fw.dma(sp, gl[k][:], gluT_d[s][:, sl].rearrange("(m p) t -> p m t", p=128), reads=[b_gluT[s]], writes=[bin_[k]])
                    fw.dma(sp, ot[k][:], o_d[s][sl, :], reads=[b_o[s]], writes=[bin_[k]])
                    fw.dma(sp, gt[k][:], gates_d[s][sl, :], reads=[b_gates[s]], writes=[bin_[k]])
                    fw.dma(sp, xt[k][:], xsrc[sl, :], reads=[b_xsrc], writes=[bin_[k]])

                def part1(i):
                    k = i % NBUF
                    k2 = i % 2
                    bi = bin_[k]
                    pt, bpt = nt()
                    for c in range(8):
                        fw.op(pe, lambda h: h.transpose(pt[:, c * 128:(c + 1) * 128], ot[k][:, c * 128:(c + 1) * 128], identb[:]), [bi, b_id], [bpt])
                    evac(act, oT[:].rearrange("p c t -> p (c t)"), pt[:], [bpt], [boT])
                    for cb in range(2):
                        cs = slice(cb * 512, (cb + 1) * 512)
                        pya, bpya = nb()
                        for c in range(3):
                            fw.op(pe, lambda h: h.matmul(pya[:], fm[k][:, c, :], wf[:, c, cs], start=(c == 0), stop=(c == 2)), [bi, bw], [bpya])
                        pyb, bpyb = nb()
                        for c in range(3):
                            fw.op(pe, lambda h: h.matmul(pyb[:], gl[k][:, c, :], ws[:, c, cs], start=(c == 0), stop=(c == 2)), [bi, bw], [bpyb])
                        pyc, bpyc = nb()
                        for c in range(8):
                            fw.op(pe, lambda h: h.matmul(pyc[:], oT[:, c, :], wo[:, c, cs], start=(c == 0), stop=(c == 7)), [boT, bw], [bpyc])
                        fw.op(dve, lambda h: h.tensor_tensor(m1[k2][:, cs], pya[:], gt[k][:, cb * 512:(cb + 1) * 512], ALU.mult), [bpya, bi], [bm1[k2]])
                        fw.op(dve, lambda h: h.tensor_tensor(m2[k2][:, cs], pyb[:], gt[k][:, D + cb * 512:D + (cb + 1) * 512], ALU.mult), [bpyb, bi], [bm2[k2]])
                        fw.op(dve, lambda h: h.tensor_tensor(m3[k2][:, cs], pyc[:], gt[k][:, 2 * D + cb * 512:2 * D + (cb + 1) * 512], ALU.mult), [bpyc, bi], [bm3[k2]])
                    fw.op(pool, lambda h: h.tensor_tensor(m1[k2][:], m1[k2][:], m2[k2][:], ALU.add), [bm1[k2], bm2[k2]], [bm1[k2]])
                    fw.op(pool, lambda h: h.tensor_tensor(mb[k2][:], m1[k2][:], m3[k2][:], ALU.add), [bm1[k2], bm3[k2]], [bmb[k2]])

                def part2(i):
                    k = i % NBUF
                    k2 = i % 2
                    bi = bin_[k]
                    pt, bpt = nt()
                    for c in range(8):
                        fw.op(pe, lambda h: h.transpose(pt[:, c * 128:(c + 1) * 128], mb[k2][:, c * 128:(c + 1) * 128], identb[:]), [bmb[k2], b_id], [bpt])
                    evac(act, mT[:].rearrange("p c t -> p (c t)"), pt[:], [bpt], [bmT])
                    for cb in range(2):
                        cs = slice(cb * 512, (cb + 1) * 512)
                        po, bpo = nb()
                        for c in range(8):
                            fw.op(pe, lambda h: h.matmul(po[:], mT[:, c, :], wout[:, c, cs], start=(c == 0), stop=(c == 7)), [bmT, bw], [bpo])
                        fw.op(dve, lambda h: h.tensor_tensor(xo[k2][:, cs], po[:], xt[k][:, cs], ALU.add), [bpo, bi], [bxo[k2]])
                    fw.dma(pool, xdst[i * 128:(i + 1) * 128, :], xo[k2][:], reads=[bxo[k2]], writes=[b_xdst])

                for (s, xsrc, b_xsrc, xdst, b_xdst) in seqs:
                    load(0)
                    if NTC > 1:
                        load(1)
                    part1(0)
                    for i in range(NTC):
                        if i + 2 < NTC:
                            load(i + 2)
                        if i + 1 < NTC:
                            part1(i + 1)
                        part2(i)
                        next(rider, None)
                for _ in rider:
                    pass
                fw.barrier()

        def phase_C2(l, seqs, wu, bwu):
            TB = 256
            with ExitStack() as ph:
                wd = T(ph, [128, 32, D], BF16, "wd"); bw = Buf()
                for c in range(32):
                    load_w(wd[:, c, :], w_down[l, c * 128:(c + 1) * 128, :], bw, D)
                gm = T(ph, [128, D], F32, "gm"); gf = T(ph, [128, D], F32, "gf"); bg = Buf()
                fw.dma(sp, gm[:], g_mlp[l].partition_broadcast(128), writes=[bg])
                fw.dma(sp, gf[:], g_final.partition_broadcast(128), writes=[bg])
                xt = [T(ph, [128, 2, D], F32, "xt") for _ in range(2)]; bxt = [Buf(), Buf()]
                hb = T(ph, [128, D], BF16, "hb"); bhb = Buf()
                junk = hb; bjunk = bhb
                ss = T(ph, [128, 2], F32, "ss"); bss = Buf()
                ss3 = T(ph, [128, 1], F32, "ss3"); bss3 = Buf()
                hTs = [T(ph, [128, 8, TB], BF16, "hT") for _ in range(2)]; bhTs = [Buf(), Buf()]
                rl = T(ph, [128, 2, TB], F32, "rl"); brl = Buf()
                aT = T(ph, [128, 32, TB], BF16, "aT"); baT = Buf()
                xo = [T(ph, [128, D], F32, "xo") for _ in range(2)]; bxo = [Buf(), Buf()]
                pT = [P(ph, [128, 1024], BF16, "pT") for _ in range(2)]; bpT = [PB(), PB()]
                pa = [P(ph, [128, 512], F32, "pa") for _ in range(6)]; bpa = [PB() for _ in range(6)]
                pai = [0]; pti = [0]; xoi = [0]

                def nb():
                    i = pai[0] % 6
                    pai[0] += 1
                    return pa[i], bpa[i]

                def nt():
                    i = pti[0] % 2
                    pti[0] += 1
                    return pT[i], bpT[i]

                nblk = L // TB

                def load(b):
                    fw.dma(sp, xt[b % 2][:], xsrc[b * TB:(b + 1) * TB, :].rearrange("(j p) d -> p j d", p=128), reads=[b_xsrc], writes=[bxt[b % 2]])

                def pre(b):
                    xi = xt[b % 2]; bxi = bxt[b % 2]
                    hT = hTs[b % 2]; bhT = bhTs[b % 2]
                    for j in range(2):
                        fw.op(act, lambda h: h.activation(junk[:], xi[:, j, :], AF.Square, accum_out=ss[:, j:j + 1]), [bxi], [bjunk, bss])
                    rstd_from_ss(None, ss, 2, D, bss, bss)
                    for j in range(2):
                        fw.op(dve, lambda h: h.scalar_tensor_tensor(hb[:], xi[:, j, :], ss[:, j:j + 1], gm[:], ALU.mult, ALU.mult), [bxi, bss, bg], [bhb])
                        pt, bpt = nt()
                        for c in range(8):
                            fw.op(pe, lambda h: h.transpose(pt[:, c * 128:(c + 1) * 128], hb[:, c * 128:(c + 1) * 128], identb[:]), [bhb, b_id], [bpt])
                        evac(act if j else dve, hT[:, :, j * 128:(j + 1) * 128], pt[:].rearrange("p (c t) -> p c t", c=8), [bpt], [bhT])

                def up(b):
                    hT = hTs[b % 2]; bhT = bhTs[b % 2]
                    for f2 in range(16):
                        ps, bps = nb()
                        for ff in range(2):
                            f = f2 * 2 + ff
                            for c in range(8):
                                fw.op(pe, lambda h: h.matmul(ps[:, ff * TB:(ff + 1) * TB], wu[:, c, f * 128:(f + 1) * 128], hT[:, c, :], start=(c == 0 and ff == 0), stop=(c == 7), skip_group_check=True), [bwu, bhT], [bps])
                        fw.op(act, lambda h: h.activation(rl[:].rearrange("p a t -> p (a t)"), ps[:], AF.Relu), [bps], [brl])
                        eng = pool if f2 % 2 else dve
                        fw.op(eng, lambda h: h.tensor_tensor(aT[:, f2 * 2:f2 * 2 + 2, :], rl[:], rl[:], ALU.mult), [brl], [baT])

                def down(b):
                    xi = xt[b % 2]; bxi = bxt[b % 2]
                    for j in range(2):
                        xk = xo[xoi[0] % 2]; bxk = bxo[xoi[0] % 2]
                        xoi[0] += 1
                        for cb in range(2):
                            cs = slice(cb * 512, (cb + 1) * 512)
                            ps, bps = nb()
                            for f in range(32):
                                fw.op(pe, lambda h: h.matmul(ps[:], aT[:, f, j * 128:(j + 1) * 128], wd[:, f, cs], start=(f == 0), stop=(f == 31)), [baT, bw], [bps])
                            fw.op(dve, lambda h: h.tensor_tensor(xk[:, cs], ps[:], xi[:, j, cs], ALU.add), [bps, bxi], [bxk])
                        if final:
                            fw.op(act, lambda h: h.activation(rl[:].rearrange("p a t -> p (a t)"), xk[:, 0:2 * TB], AF.Square, accum_out=ss3[:, 0:1]), [bxk], [brl, bss3])
                            fw.op(act, lambda h: h.activation(rl[:].rearrange("p a t -> p (a t)"), xk[:, 2 * TB:4 * TB], AF.Square, accum_out=ss[:, 0:1]), [bxk], [brl, bss])
                            fw.op(dve, lambda h: h.tensor_tensor(ss3[:], ss3[:], ss[:, 0:1], ALU.add), [bss3, bss], [bss3])
                            rstd_from_ss(None, ss3, 1, D, bss3, bss3)
                            fw.op(dve, lambda h: h.scalar_tensor_tensor(xk[:], xk[:], ss3[:, 0:1], gf[:], ALU.mult, ALU.mult), [bxk, bss3, bg], [bxk])
                        r0 = b * TB + j * 128
                        fw.dma(pool, xdst[r0:r0 + 128, :], xk[:], reads=[bxk], writes=[b_xdst])

                for (s, xsrc, b_xsrc, xdst, b_xdst, final) in seqs:
                    load(0)
                    pre(0)
                    for b in range(nblk):
                        if b + 1 < nblk:
                            load(b + 1)
                        up(b)
                        if b + 1 < nblk:
                            pre(b + 1)
                        down(b)
                fw.barrier()

        fw.barrier()
        done = False
        for l in range(DEPTH):
            last = (l == DEPTH - 1)
            if l == 0:
                phase_A(l, [(s, x_in[s], b_const) for s in range(NS)])
            else:
                phase_A(l, [(s, xb_d[s], b_xb[s]) for s in range(NS)])
            if stop_after == "A":
                break
            for s in range(NS):
                phase_F(l, s)
            if stop_after == "F":
                break
            phase_S(l)
            if stop_after == "S":
                break
            for s in range(NS):
                phase_M(l, s)
            if stop_after == "M":
                break
            with ExitStack() as c12:
                wu_t = T(c12, [128, 8, DFF], BF16, "wu"); bwu_t = Buf()

                def wu_rider():
                    for c in range(8):
                        yield from load_w_gen(wu_t[:, c, :], w_up[l, c * 128:(c + 1) * 128, :], bwu_t, DFF)

                phase_C1(l, [((s, x_in[s], b_const) if l == 0 else (s, xb_d[s], b_xb[s])) + (xa_d[s], b_xa[s]) for s in range(NS)], wu_rider())
                if stop_after == "C1":
                    break
                if last:
                    phase_C2(l, [(s, xa_d[s], b_xa[s], y_out[s], b_y[s], True) for s in range(NS)], wu_t, bwu_t)
                else:
                    phase_C2(l, [(s, xa_d[s], b_xa[s], xb_d[s], b_xb[s], False) for s in range(NS)], wu_t, bwu_t)
        allb = b_y + b_ygT + b_zf + b_zs5T + b_cqnT + b_ckvnT + b_kpeT + b_gates + b_fmT + b_gluT + b_o + b_xa + b_xb
        fw.finish(allb)
        fw.barrier()
        stats = {e.name: (e.nins, e.nwait) for e in fw.engs}
    return nc, stats


def make_consts(L):
    bf = ml_dtypes.bfloat16
    c = {}
    c["c_identf"] = np.eye(128, dtype=np.float32)
    js = np.zeros((128, 128), np.float32)
    for k in range(128):
        js[k, (k + 64) % 128] = 1.0
    c["c_jswap"] = js
    misc = np.zeros((128, 16), np.float32)
    misc[:64, 0] = 1.0
    misc[64:, 0] = -1.0
    misc[:, 1] = -misc[:, 0]
    for e in range(8):
        misc[e * 16:(e + 1) * 16, 8 + e] = 1.0
    c["c_misc"] = misc
    t = np.arange(L, dtype=np.int64)
    tk = (t[:, None] * t[None, :]) % L
    ang = 2.0 * np.pi * tk.astype(np.float64) / L
    dft = np.empty((L, 2, L), dtype=bf)
    dft[:, 0, :] = (np.cos(ang) / math.sqrt(L)).astype(bf)
    dft[:, 1, :] = (np.sin(ang) / math.sqrt(L)).astype(bf)
    c["c_dft"] = dft
    a = np.arange(64)
    ang64 = 2.0 * np.pi * ((a[:, None] * a[None, :]) % 64) / 64.0
    c64 = np.zeros((128, 2, 128), np.float64)
    for blk in range(2):
        sl = slice(blk * 64, (blk + 1) * 64)
        c64[sl, 0, sl] = np.cos(ang64) / 8.0
        c64[sl, 1, sl] = -np.sin(ang64) / 8.0
    c["c_c64"] = c64.astype(bf)
    half = 16
    inv = 10000.0 ** (-np.arange(half, dtype=np.float32) / half)
    angr = np.arange(L, dtype=np.float32)[:, None] * inv[None, :]
    cos = np.cos(angr).T
    sin = np.sin(angr).T
    rope = np.zeros((2, 32, L), np.float32)
    rope[0, :16] = cos
    rope[0, 16:] = cos
    rope[1, :16] = -sin
    rope[1, 16:] = sin
    c["c_rope"] = rope
    return c


WNAMES = ["g_mix", "w_in", "w_fnet", "s5_lam_re", "s5_lam_im", "s5_log_dt", "s5_b_re", "s5_b_im", "s5_c_re",
          "s5_c_im", "s5_d", "w_glu", "w_s5", "g_q", "w_qb", "g_kv", "w_kvb", "w_o_mla", "w_out", "g_mlp",
          "w_up", "w_down", "g_final"]

_CACHE = {}


def kernel(**inputs):
    xp = np.asarray(inputs["x_prompt"], dtype=np.float32)
    xs = np.asarray(inputs["x_sample"], dtype=np.float32)
    L = xp.shape[1]
    DEPTH = inputs["w_in"].shape[0]
    NS = 2
    ncores = 8
    seqs = [xp[i] for i in range(xp.shape[0])] + [xs[i] for i in range(xs.shape[0])]
    nseq = len(seqs)
    assign = [[i, i + ncores] for i in range(ncores)]
    key = (L, NS, DEPTH)
    if key not in _CACHE:
        _CACHE[key] = build_program(L, NS, DEPTH)[0]
    nc = _CACHE[key]
    consts = make_consts(L)
    wmap = {n: np.ascontiguousarray(np.asarray(inputs[n], dtype=np.float32)) for n in WNAMES}
    in_maps = []
    for ci in range(ncores):
        xcore = np.zeros((NS, L, D), np.float32)
        for k, si in enumerate(assign[ci]):
            if si < nseq:
                xcore[k] = seqs[si]
        m = {"x": xcore}
        m.update(wmap)
        m.update(consts)
        in_maps.append(m)
    res = run_bass_kernel_spmd(nc, in_maps, core_ids=list(range(ncores)))
    outs = [None] * nseq
    for ci in range(ncores):
        y = res.results[ci]["y"]
        for k, si in enumerate(assign[ci]):
            if si < nseq:
                outs[si] = y[k]
    y_prompt = np.stack(outs[:xp.shape[0]], axis=0).astype(np.float32)
    y_sample = np.stack(outs[xp.shape[0]:], axis=0).astype(np.float32)
    return (y_prompt, y_sample)
```

```python
import math
import os
from contextlib import ExitStack

import numpy as np
import ml_dtypes

import concourse.bass as bass
import concourse.mybir as mybir
from concourse.bass_utils import run_bass_kernel_spmd

F32 = mybir.dt.float32
BF16 = mybir.dt.bfloat16
I32 = mybir.dt.int32
AF = mybir.ActivationFunctionType
ALU = mybir.AluOpType

D = 1024
FW_ = 384
S5W = 384
QL = 384
KVL = 256
NH = 16
OFF_FNET = 0
OFF_S5 = 384
OFF_Q = 768
OFF_KV = 1152
OFF_KR = 1408
OFF_GATE = 1440
IN_W = 4512
DFF = 4096
EPS = 1e-6
TST = 64
SCALE = 96 ** -0.5

SEM_ROLL = 30000
S_STOP = 0
NDSEM = 6


class Buf:
    __slots__ = ("w", "r", "x")

    def __init__(self, x=False):
        self.w = {}
        self.r = {}
        self.x = x


def PB():
    return Buf(True)


class Eng:
    def __init__(self, name, h):
        self.name = name
        self.h = h
        self.sem = None
        self.semkey = None
        self.cnt = 0
        self.own = set()
        self.waited = {}
        self.dsems = []
        self.dcnt = 0
        self.nins = 0
        self.nwait = 0
        self.last = {}


class FW:
    def __init__(self, nc, stack):
        self.nc = nc
        self.stack = stack
        self.semtab = {}
        self.nsem = 0
        self.pe = Eng("pe", nc.tensor)
        self.act = Eng("act", nc.scalar)
        self.dve = Eng("dve", nc.vector)
        self.pool = Eng("pool", nc.gpsimd)
        self.sp = Eng("sp", nc.sync)
        self.engs = [self.pe, self.act, self.dve, self.pool, self.sp]
        for e in self.engs:
            self._newsem(e)
        for e in (self.sp, self.pool, self.act):
            for i in range(NDSEM):
                e.dsems.append(self._mksem(f"d_{e.name}_{i}"))

    def _mksem(self, name):
        s = self.stack.enter_context(self.nc.semaphore(name))
        key = self.nsem
        self.nsem += 1
        self.semtab[key] = s
        return key

    def _newsem(self, e):
        key = self._mksem(f"c_{e.name}_{len(e.own)}")
        e.sem = self.semtab[key]
        e.semkey = key
        e.cnt = 0
        e.own.add(key)

    def _wait(self, e, deps):
        for key, val in deps.items():
            if e.waited.get(key, 0) >= val:
                continue
            e.h.wait_ge(self.semtab[key], val)
            e.waited[key] = val
            e.nwait += 1

    def op(self, e, fn, reads=(), writes=(), dma=False):
        deps = {}
        own = e.own
        ispe = e is self.pe
        for b in reads:
            for k, v in b.w.items():
                if ispe and k in own:
                    continue
                if deps.get(k, 0) < v:
                    deps[k] = v
            if b.x:
                for k, v in b.r.items():
                    if k in own:
                        continue
                    if deps.get(k, 0) < v:
                        deps[k] = v
        for b in writes:
            for k, v in b.w.items():
                if k in own:
                    continue
                if deps.get(k, 0) < v:
                    deps[k] = v
            for k, v in b.r.items():
                if k in own:
                    continue
                if deps.get(k, 0) < v:
                    deps[k] = v
        if dma:
            slot = e.dcnt % NDSEM
            use = e.dcnt // NDSEM
            key = e.dsems[slot]
            if use > 0 and deps.get(key, 0) < 16 * use:
                deps[key] = 16 * use
            self._wait(e, deps)
            ins = fn(e.h)
            ins.then_inc(self.semtab[key], 16)
            e.dcnt += 1
            ev = (key, 16 * (use + 1))
        else:
            self._wait(e, deps)
            if e.cnt >= SEM_ROLL:
                self._newsem(e)
            ins = fn(e.h)
            ins.then_inc(e.sem, 1)
            e.cnt += 1
            ev = (e.semkey, e.cnt)
        e.nins += 1
        k, v = ev
        e.last[k] = v
        for b in reads:
            if b.r.get(k, 0) < v:
                b.r[k] = v
        for b in writes:
            if b.w.get(k, 0) < v:
                b.w[k] = v
        return ev

    def dma(self, e, out, in_, reads=(), writes=()):
        return self.op(e, lambda h: h.dma_start(out=out, in_=in_), reads, writes, dma=True)

    def barrier(self):
        deps = {}
        for e in self.engs:
            for k, v in e.last.items():
                if deps.get(k, 0) < v:
                    deps[k] = v
        for e in self.engs:
            d = {k: v for k, v in deps.items() if k not in e.own}
            self._wait(e, d)

    def finish(self, bufs):
        deps = {}
        for b in bufs:
            for k, v in b.w.items():
                if deps.get(k, 0) < v:
                    deps[k] = v
        self._wait(self.sp, deps)


def build_program(L, NS, DEPTH, dbg=False, stop_after=None):
    nc = bass.Bass("TRN2", target_bir_lowering=False)
    NTC = L // 128
    NB = L // 512
    NC_ = L // TST
    NC2 = NS * NC_
    uid = [0]

    def dram(name, shape, dt, kind="Internal"):
        return nc.dram_tensor(name, list(shape), dt, kind=kind).ap()

    def din(name, shape, dt=F32):
        return dram(name, shape, dt, kind="ExternalInput")

    x_in = din("x", [NS, L, D])
    g_mix = din("g_mix", [DEPTH, D])
    w_in = din("w_in", [DEPTH, D, IN_W])
    w_fnet = din("w_fnet", [DEPTH, FW_, D])
    lam_re = din("s5_lam_re", [DEPTH, 2, 24, 64])
    lam_im = din("s5_lam_im", [DEPTH, 2, 24, 64])
    log_dt = din("s5_log_dt", [DEPTH, 2, 24])
    b_re = din("s5_b_re", [DEPTH, 2, 24, 64, 16])
    b_im = din("s5_b_im", [DEPTH, 2, 24, 64, 16])
    c_re = din("s5_c_re", [DEPTH, 2, 24, 16, 64])
    c_im = din("s5_c_im", [DEPTH, 2, 24, 16, 64])
    s5_d = din("s5_d", [DEPTH, S5W])
    w_glu = din("w_glu", [DEPTH, S5W, 2 * S5W])
    w_s5 = din("w_s5", [DEPTH, S5W, D])
    g_q = din("g_q", [DEPTH, QL])
    w_qb = din("w_qb", [DEPTH, QL, NH * 96])
    g_kv = din("g_kv", [DEPTH, KVL])
    w_kvb = din("w_kvb", [DEPTH, KVL, NH * 128])
    w_o = din("w_o_mla", [DEPTH, D, D])
    w_out = din("w_out", [DEPTH, D, D])
    g_mlp = din("g_mlp", [DEPTH, D])
    w_up = din("w_up", [DEPTH, D, DFF])
    w_down = din("w_down", [DEPTH, DFF, D])
    g_final = din("g_final", [D])
    c_identf = din("c_identf", [128, 128])
    c_jswap = din("c_jswap", [128, 128])
    c_misc = din("c_misc", [128, 16])
    c_dft = din("c_dft", [L, 2, L], BF16)
    c_c64 = din("c_c64", [128, 2, 128], BF16)
    c_rope = din("c_rope", [2, 32, L])

    okind = "ExternalOutput"
    y_out = dram("y", [NS, L, D], F32, kind=okind)
    skind = okind if dbg else "Internal"
    zf_d = dram("zf", [NS, L, FW_], BF16, kind=skind)
    zs5T_d = dram("zs5T", [NS, S5W, L], BF16, kind=skind)
    cqnT_d = dram("cqnT", [NS, QL, L], BF16, kind=skind)
    ckvnT_d = dram("ckvnT", [NS, KVL, L], BF16, kind=skind)
    kpeT_d = dram("kpeT", [NS, 32, L], BF16, kind=skind)
    gates_d = dram("gates", [NS, L, 3 * D], BF16, kind=skind)
    fmT_d = dram("fmT", [NS, FW_, L], BF16, kind=skind)
    gluT_d = dram("gluT", [NS, S5W, L], BF16, kind=skind)
    ygT_d = dram("ygT", [NS, S5W, L], BF16, kind=skind)
    o_d = dram("o_att", [NS, L, D], BF16, kind=skind)
    xa_d = dram("xa", [NS, L, D], F32, kind=skind)
    xb_d = dram("xb", [NS, L, D], F32, kind=skind)

    mkb = lambda: [Buf() for _ in range(NS)]
    b_zf, b_zs5T, b_cqnT, b_ckvnT, b_kpeT, b_gates = mkb(), mkb(), mkb(), mkb(), mkb(), mkb()
    b_fmT, b_gluT, b_o, b_xa, b_xb, b_y = mkb(), mkb(), mkb(), mkb(), mkb(), mkb()
    b_const = Buf()
    b_ygT = mkb()

    with ExitStack() as top:
        fw = FW(nc, top)
        pe, act, dve, pool, sp = fw.pe, fw.act, fw.dve, fw.pool, fw.sp

        def T(st, shape, dt, name="t"):
            uid[0] += 1
            return st.enter_context(nc.sbuf_tensor(f"{name}_{uid[0]}", list(shape), dt))

        def P(st, shape, dt, name="p"):
            uid[0] += 1
            return st.enter_context(nc.psum_tensor(f"{name}_{uid[0]}", list(shape), dt))

        identf = T(top, [128, 128], F32, "identf")
        identb = T(top, [128, 128], BF16, "identb")
        jswap = T(top, [128, 128], F32, "jswap")
        misc = T(top, [128, 16], F32, "misc")
        stg = [T(top, [128, 1152], F32, "stg") for _ in range(2)]
        b_stg = [Buf(), Buf()]
        b_id = Buf()
        fw.dma(sp, identf[:], c_identf, writes=[b_id])
        fw.dma(sp, jswap[:], c_jswap, writes=[b_id])
        fw.dma(sp, misc[:], c_misc, writes=[b_id])
        fw.op(dve, lambda h: h.tensor_copy(identb[:], identf[:]), [b_id], [b_id])
        stgi = [0]

        def load_w(dst, src, bdst, n):
            c0 = 0
            while c0 < n:
                cw = min(1152, n - c0)
                i = stgi[0] % 2
                stgi[0] += 1
                fw.dma(sp, stg[i][:, 0:cw], src[:, c0:c0 + cw], writes=[b_stg[i]])
                eng = pool if (stgi[0] % 2) else dve
                fw.op(eng, lambda h: h.tensor_copy(dst[:, c0:c0 + cw], stg[i][:, 0:cw]), [b_stg[i]], [bdst])
                c0 += cw

        def load_w_gen(dst, src, bdst, n):
            c0 = 0
            while c0 < n:
                cw = min(1152, n - c0)
                i = stgi[0] % 2
                stgi[0] += 1
                fw.dma(sp, stg[i][:, 0:cw], src[:, c0:c0 + cw], writes=[b_stg[i]])
                fw.op(act, lambda h: h.activation(dst[:, c0:c0 + cw], stg[i][:, 0:cw], AF.Copy), [b_stg[i]], [bdst])
                c0 += cw
                yield

        def evac(eng, out, in_, reads, writes):
            if eng is act:
                return fw.op(act, lambda h: h.activation(out, in_, AF.Copy), reads, writes)
            return fw.op(eng, lambda h: h.tensor_copy(out, in_), reads, writes)

        def rstd_from_ss(st_tiles, ss, n, width, reads_b, b_out):
            fw.op(dve, lambda h: h.tensor_scalar(ss[:, 0:n], ss[:, 0:n], 1.0 / width, EPS, ALU.mult, ALU.add), [reads_b], [b_out])
            fw.op(act, lambda h: h.activation(ss[:, 0:n], ss[:, 0:n], AF.Sqrt), [b_out], [b_out])
            fw.op(dve, lambda h: h.reciprocal(ss[:, 0:n], ss[:, 0:n]), [b_out], [b_out])

        def phase_A(l, seqs):
            with ExitStack() as ph:
                w = T(ph, [128, 8, IN_W], BF16, "w_in"); bw = Buf()
                for c in range(8):
                    load_w(w[:, c, :], w_in[l, c * 128:(c + 1) * 128, :], bw, IN_W)
                wsw = T(ph, [128, 8, 96], BF16, "wsw"); bwsw = Buf()
                fw.op(pool, lambda h: h.tensor_copy(wsw[:, :, 0:64], w[:, :, OFF_KR - 64:OFF_KR]), [bw], [bwsw])
                fw.op(pool, lambda h: h.tensor_copy(wsw[:, :, 64:80], w[:, :, OFF_KR + 16:OFF_KR + 32]), [bw], [bwsw])
                fw.op(pool, lambda h: h.tensor_copy(wsw[:, :, 80:96], w[:, :, OFF_KR:OFF_KR + 16]), [bw], [bwsw])
                gmix = T(ph, [128, D], F32, "gmix"); gq = T(ph, [128, QL], F32, "gq"); gkv = T(ph, [128, KVL], F32, "gkv")
                bg = Buf()
                fw.dma(sp, gmix[:], g_mix[l].partition_broadcast(128), writes=[bg])
                fw.dma(sp, gq[:], g_q[l].partition_broadcast(128), writes=[bg])
                fw.dma(sp, gkv[:], g_kv[l].partition_broadcast(128), writes=[bg])
                xt = [T(ph, [128, 4, D], F32, "xt") for _ in range(2)]; bxt = [Buf(), Buf()]
                rc = [T(ph, [96, 2, 512], F32, "rc") for _ in range(2)]; brc = [Buf(), Buf()]
                hb = T(ph, [128, D], BF16, "hb"); bhb = Buf()
                junk = T(ph, [128, D], BF16, "junk"); bjunk = Buf()
                hTs = [T(ph, [128, 8, 512], BF16, "hT") for _ in range(2)]; bhTs = [Buf(), Buf()]
                ss = T(ph, [128, 4], F32, "ss"); bss = Buf()
                ss2 = T(ph, [128, 2], F32, "ss2"); bss2 = Buf()
                zs_t = T(ph, [128, 3, 512], BF16, "zs_t"); bzs = Buf()
                zf_t = T(ph, [128, 4, FW_], BF16, "zf_t"); bzf = Buf()
                cn = T(ph, [128, QL + KVL], BF16, "cn"); bcn = Buf()
                cT_t = T(ph, [128, 5, 512], BF16, "cT_t"); bcT = Buf()
                g_t = [T(ph, [128, 3 * D], BF16, "g_t") for _ in range(2)]; bgt = [Buf(), Buf()]
                t1 = T(ph, [96, 512], F32, "t1"); t2 = T(ph, [96, 512], F32, "t2"); bt1 = Buf(); bt2 = Buf()
                kp_t = T(ph, [96, 512], BF16, "kp_t"); bkp = Buf()
                pT = [P(ph, [128, 1024], BF16, "pT") for _ in range(2)]; bpT = [PB(), PB()]
                pa = [P(ph, [128, 512], F32, "pa") for _ in range(6)]; bpa = [PB() for _ in range(6)]
                pai = [0]; pti = [0]; evi = [0]

                def nb():
                    i = pai[0] % 6
                    pai[0] += 1
                    return pa[i], bpa[i]

                def nt():
                    i = pti[0] % 2
                    pti[0] += 1
                    return pT[i], bpT[i]

                def ev_eng():
                    evi[0] += 1
                    return act if evi[0] % 2 else dve

                def load_blk(b):
                    i = b % 2
                    fw.dma(sp, xt[i][:], xsrc[b * 512:(b + 1) * 512, :].rearrange("(j p) d -> p j d", p=128),
                           reads=[b_xsrc], writes=[bxt[i]])
                    fw.dma(sp, rc[i][64:96, :, :], c_rope[:, :, b * 512:(b + 1) * 512].rearrange("a d t -> d a t"),
                           writes=[brc[i]])

                def pre(b):
                    xi = xt[b % 2]; bxi = bxt[b % 2]
                    hT = hTs[b % 2]; bhT = bhTs[b % 2]
                    for j in range(4):
                        fw.op(act, lambda h: h.activation(junk[:], xi[:, j, :], AF.Square, accum_out=ss[:, j:j + 1]), [bxi], [bjunk, bss])
                    rstd_from_ss(None, ss, 4, D, bss, bss)
                    for j in range(4):
                        fw.op(dve, lambda h: h.scalar_tensor_tensor(hb[:], xi[:, j, :], ss[:, j:j + 1], gmix[:], ALU.mult, ALU.mult),
                              [bxi, bss, bg], [bhb])
                        pt, bpt = nt()
                        for c in range(8):
                            fw.op(pe, lambda h: h.transpose(pt[:, c * 128:(c + 1) * 128], hb[:, c * 128:(c + 1) * 128], identb[:]), [bhb, b_id], [bpt])
                        evac(ev_eng(), hT[:, :, j * 128:(j + 1) * 128], pt[:].rearrange("p (c t) -> p c t", c=8), [bpt], [bhT])

                def main(b):
                    rci = rc[b % 2]; brci = brc[b % 2]
                    hT = hTs[b % 2]; bhT = bhTs[b % 2]
                    for m in range(3):
                        ps, bps = nb()
                        for c in range(8):
                            fw.op(pe, lambda h: h.matmul(ps[:], w[:, c, OFF_S5 + m * 128:OFF_S5 + (m + 1) * 128], hT[:, c, :], start=(c == 0), stop=(c == 7)), [bw, bhT], [bps])
                        evac(ev_eng(), zs_t[:, m, :], ps[:], [bps], [bzs])
                    fw.dma(pool, zs5T_d[s][:, b * 512:(b + 1) * 512].rearrange("(m p) t -> p m t", p=128), zs_t[:], reads=[bzs], writes=[b_zs5T[s]])
                    ps, bps = nb()
                    ps2, bps2 = nb()
                    for c in range(8):
                        fw.op(pe, lambda h: h.matmul(ps[0:96, :], w[:, c, OFF_KR - 64:OFF_KR + 32], hT[:, c, :], start=(c == 0), stop=(c == 7)), [bw, bhT], [bps])
                    for c in range(8):
                        fw.op(pe, lambda h: h.matmul(ps2[0:96, :], wsw[:, c, :], hT[:, c, :], start=(c == 0), stop=(c == 7)), [bwsw, bhT], [bps2])
                    fw.op(dve, lambda h: h.tensor_tensor(t1[64:96, :], ps[64:96, :], rci[64:96, 0, :], ALU.mult), [bps, brci], [bt1])
                    fw.op(dve, lambda h: h.tensor_tensor(t2[64:96, :], ps2[64:96, :], rci[64:96, 1, :], ALU.mult), [bps2, brci], [bt2])
                    fw.op(pool, lambda h: h.tensor_tensor(kp_t[64:96, :], t1[64:96, :], t2[64:96, :], ALU.add), [bt1, bt2], [bkp])
                    fw.dma(pool, kpeT_d[s][:, b * 512:(b + 1) * 512], kp_t[64:96, :], reads=[bkp], writes=[b_kpeT[s]])
                    for j in range(4):
                        if j == 2:
                            yield
                        lhs = lambda c: hT[:, c, j * 128:(j + 1) * 128]
                        ps, bps = nb()
                        for c in range(8):
                            fw.op(pe, lambda h: h.matmul(ps[:, 0:FW_], lhs(c), w[:, c, OFF_FNET:OFF_FNET + FW_], start=(c == 0), stop=(c == 7)), [bw, bhT], [bps])
                        evac(ev_eng(), zf_t[:, j, :], ps[:, 0:FW_], [bps], [bzf])
                        psq, bpsq = nb()
                        for c in range(8):
                            fw.op(pe, lambda h: h.matmul(psq[:, 0:QL], lhs(c), w[:, c, OFF_Q:OFF_Q + QL], start=(c == 0), stop=(c == 7)), [bw, bhT], [bpsq])
                        psk, bpsk = nb()
                        for c in range(8):
                            fw.op(pe, lambda h: h.matmul(psk[:, 0:KVL], lhs(c), w[:, c, OFF_KV:OFF_KV + KVL], start=(c == 0), stop=(c == 7)), [bw, bhT], [bpsk])
                        fw.op(act, lambda h: h.activation(junk[:, 0:QL], psq[:, 0:QL], AF.Square, accum_out=ss2[:, 0:1]), [bpsq], [bjunk, bss2])
                        fw.op(act, lambda h: h.activation(junk[:, 0:KVL], psk[:, 0:KVL], AF.Square, accum_out=ss2[:, 1:2]), [bpsk], [bjunk, bss2])
                        fw.op(dve, lambda h: h.tensor_scalar(ss2[:, 0:1], ss2[:, 0:1], 1.0 / QL, EPS, ALU.mult, ALU.add), [bss2], [bss2])
                        fw.op(dve, lambda h: h.tensor_scalar(ss2[:, 1:2], ss2[:, 1:2], 1.0 / KVL, EPS, ALU.mult, ALU.add), [bss2], [bss2])
                        fw.op(act, lambda h: h.activation(ss2[:], ss2[:], AF.Sqrt), [bss2], [bss2])
                        fw.op(dve, lambda h: h.reciprocal(ss2[:], ss2[:]), [bss2], [bss2])
                        fw.op(dve, lambda h: h.scalar_tensor_tensor(cn[:, 0:QL], psq[:, 0:QL], ss2[:, 0:1], gq[:], ALU.mult, ALU.mult), [bpsq, bss2, bg], [bcn])
                        fw.op(dve, lambda h: h.scalar_tensor_tensor(cn[:, QL:QL + KVL], psk[:, 0:KVL], ss2[:, 1:2], gkv[:], ALU.mult, ALU.mult), [bpsk, bss2, bg], [bcn])
                        gt = g_t[j % 2]; bg_t = bgt[j % 2]
                        for q in range(6):
                            ps, bps = nb()
                            for c in range(8):
                                fw.op(pe, lambda h: h.matmul(ps[:], lhs(c), w[:, c, OFF_GATE + q * 512:OFF_GATE + (q + 1) * 512], start=(c == 0), stop=(c == 7)), [bw, bhT], [bps])
                            fw.op(act, lambda h: h.activation(gt[:, q * 512:(q + 1) * 512], ps[:], AF.Sigmoid), [bps], [bg_t])
                        r0 = b * 512 + j * 128
                        fw.dma(pool, gates_d[s][r0:r0 + 128, :], gt[:], reads=[bg_t], writes=[b_gates[s]])
                        pt, bpt = nt()
                        for c in range(5):
                            fw.op(pe, lambda h: h.transpose(pt[:, c * 128:(c + 1) * 128], cn[:, c * 128:(c + 1) * 128], identb[:]), [bcn, b_id], [bpt])
                        evac(ev_eng(), cT_t[:, :, j * 128:(j + 1) * 128], pt[:, 0:640].rearrange("p (c t) -> p c t", c=5), [bpt], [bcT])
                    fw.dma(pool, zf_d[s][b * 512:(b + 1) * 512, :].rearrange("(j p) f -> p j f", p=128), zf_t[:], reads=[bzf], writes=[b_zf[s]])
                    fw.dma(pool, cqnT_d[s][:, b * 512:(b + 1) * 512].rearrange("(m p) t -> p m t", p=128), cT_t[:, 0:3, :], reads=[bcT], writes=[b_cqnT[s]])
                    fw.dma(pool, ckvnT_d[s][:, b * 512:(b + 1) * 512].rearrange("(m p) t -> p m t", p=128), cT_t[:, 3:5, :], reads=[bcT], writes=[b_ckvnT[s]])
                for (s, xsrc, b_xsrc) in seqs:
                    load_blk(0)
                    pre(0)
                    for b in range(NB):
                        if b + 1 < NB:
                            load_blk(b + 1)
                        g = main(b)
                        next(g)
                        if b + 1 < NB:
                            pre(b + 1)
                        for _ in g:
                            pass
                fw.barrier()

        def phase_F(l, s):
            TCH = min(16, NTC)
            NHF = NTC // TCH
            with ExitStack() as ph:
                zf = T(ph, [128, NTC, FW_], BF16, "zf_sb"); bz = Buf()
                fw.dma(sp, zf[:], zf_d[s].rearrange("(c p) f -> p c f", p=128), reads=[b_zf[s]], writes=[bz])
                c64 = T(ph, [128, 2, 128], BF16, "c64"); bc64 = Buf()
                fw.dma(sp, c64[:], c_c64, writes=[bc64])
                dt_ = [T(ph, [128, TCH, 512], BF16, "dft") for _ in range(3)]; bdt = [Buf() for _ in range(3)]
                pq = T(ph, [128, 2, 3, 512], BF16, "pq"); bpq = Buf()
                fm_t = [T(ph, [128, 3, 512], BF16, "fm_t") for _ in range(2)]; bfm = [Buf(), Buf()]
                acc = [[P(ph, [128, 512], F32, "acc") for _ in range(3)] for _ in range(2)]
                bacc = [[PB() for _ in range(3)] for _ in range(2)]
                pf = P(ph, [128, 512], F32, "pf"); bpf = PB()
                tiles = [(kb, hf, cs) for kb in range(NB) for hf in range(NHF) for cs in range(2)]

                def load_tile(i):
                    kb, hf, cs = tiles[i]
                    src = c_dft[hf * TCH * 128:(hf + 1) * TCH * 128, cs, kb * 512:(kb + 1) * 512].rearrange("(c p) k -> p c k", p=128)
                    fw.dma(sp, dt_[i % 3][:], src, writes=[bdt[i % 3]])

                load_tile(0)
                if len(tiles) > 1:
                    load_tile(1)
                for i, (kb, hf, cs) in enumerate(tiles):
                    if i + 2 < len(tiles):
                        load_tile(i + 2)
                    dti = dt_[i % 3]; bdti = bdt[i % 3]
                    for tc in range(TCH):
                        for m in range(3):
                            fw.op(pe, lambda h: h.matmul(acc[cs][m][:], zf[:, hf * TCH + tc, m * 128:(m + 1) * 128], dti[:, tc, :],
                                                         start=(hf == 0 and tc == 0), stop=(hf == NHF - 1 and tc == TCH - 1)), [bz, bdti], [bacc[cs][m]])
                    if hf == NHF - 1 and cs == 1:
                        k = 0
                        for c2 in range(2):
                            for m in range(3):
                                evac(act if k % 2 else dve, pq[:, c2, m, :], acc[c2][m][:], [bacc[c2][m]], [bpq])
                                k += 1
                        fmi = fm_t[kb % 2]; bfmi = bfm[kb % 2]
                        for m in range(3):
                            fw.op(pe, lambda h: h.matmul(pf[:], c64[:, 0, :], pq[:, 0, m, :], start=True, stop=False), [bc64, bpq], [bpf])
                            fw.op(pe, lambda h: h.matmul(pf[:], c64[:, 1, :], pq[:, 1, m, :], start=False, stop=True), [bc64, bpq], [bpf])
                            evac(act if m % 2 else dve, fmi[:, m, :], pf[:], [bpf], [bfmi])
                        fw.dma(pool, fmT_d[s][:, kb * 512:(kb + 1) * 512].rearrange("(m p) t -> p m t", p=128), fmi[:], reads=[bfmi], writes=[b_fmT[s]])
                fw.barrier()

        def sincos(ph, ang, n, want_cos, bang, name):
            TWO_PI = 2.0 * math.pi
            kf = T(ph, [128, n], F32, name + "kf"); ki = T(ph, [128, n], I32, name + "ki")
            r = T(ph, [128, n], F32, name + "r"); m = T(ph, [128, n], F32, name + "m")
            out = T(ph, [128, n], F32, name + "o")
            bb = Buf()
            shift = (math.pi / 2.0) if want_cos else 0.0
            fw.op(dve, lambda h: h.tensor_scalar(r[:], ang[:], shift, None, ALU.add), [bang], [bb])
            fw.op(dve, lambda h: h.tensor_scalar(kf[:], r[:], 1.0 / TWO_PI, None, ALU.mult), [bb], [bb])
            fw.op(dve, lambda h: h.tensor_copy(ki[:], kf[:]), [bb], [bb])
            fw.op(dve, lambda h: h.tensor_copy(kf[:], ki[:]), [bb], [bb])
            fw.op(dve, lambda h: h.scalar_tensor_tensor(r[:], kf[:], -TWO_PI, r[:], ALU.mult, ALU.add), [bb], [bb])
            fw.op(dve, lambda h: h.tensor_scalar(m[:], r[:], math.pi, TWO_PI, ALU.is_gt, ALU.mult), [bb], [bb])
            fw.op(dve, lambda h: h.tensor_tensor(r[:], r[:], m[:], ALU.subtract), [bb], [bb])
            fw.op(dve, lambda h: h.tensor_scalar(m[:], r[:], -math.pi, TWO_PI, ALU.is_lt, ALU.mult), [bb], [bb])
            fw.op(dve, lambda h: h.tensor_tensor(r[:], r[:], m[:], ALU.add), [bb], [bb])
            fw.op(act, lambda h: h.activation(out[:], r[:], AF.Sin), [bb], [bb])
            return out, bb

        def phase_S(l):
            with ExitStack() as ph:
                bp = Buf()
                LR = T(ph, [128, 48], F32, "LR"); LI = T(ph, [128, 48], F32, "LI"); DT = T(ph, [128, 48], F32, "DT")
                NAT = T(ph, [48, 2, 128], F32, "NAT"); bnat = Buf()
                for half in range(2):
                    fw.dma(sp, NAT[:, 0, half * 64:(half + 1) * 64], lam_re[l].rearrange("d g n -> (d g) n"), writes=[bnat])
                    fw.dma(sp, NAT[:, 1, half * 64:(half + 1) * 64], lam_im[l].rearrange("d g n -> (d g) n"), writes=[bnat])
                with nc.psum_tensor(f"pnat_{l}", [128, 512], F32) as pnat:
                    bpn = Buf()
                    fw.op(pe, lambda h: h.transpose(pnat[:, 0:48], NAT[:, 0, :], identf[0:48, 0:48]), [bnat, b_id], [bpn])
                    fw.op(pe, lambda h: h.transpose(pnat[:, 64:112], NAT[:, 1, :], identf[0:48, 0:48]), [bnat, b_id], [bpn])
                    fw.op(dve, lambda h: h.tensor_copy(LR[:], pnat[:, 0:48]), [bpn], [bp])
                    fw.op(dve, lambda h: h.tensor_copy(LI[:], pnat[:, 64:112]), [bpn], [bp])
                    fw.barrier()
                fw.dma(sp, DT[:], log_dt[l].rearrange("d g -> (d g)").partition_broadcast(128), writes=[bp])
                fw.op(act, lambda h: h.activation(DT[:], DT[:], AF.Exp), [bp], [bp])
                lr = T(ph, [128, 48], F32, "lr"); li = T(ph, [128, 48], F32, "li")
                fw.op(dve, lambda h: h.tensor_tensor(lr[:], LR[:], DT[:], ALU.mult), [bp], [bp])
                fw.op(dve, lambda h: h.tensor_tensor(li[:], LI[:], DT[:], ALU.mult), [bp], [bp])
                li64 = T(ph, [128, 48], F32, "li64")
                fw.op(dve, lambda h: h.tensor_scalar(li64[:], li[:], float(TST), None, ALU.mult), [bp], [bp])
                sn, b1 = sincos(ph, li, 48, False, bp, "s1")
                cs_, b2 = sincos(ph, li, 48, True, bp, "c1")
                sn64, b3 = sincos(ph, li64, 48, False, bp, "s2")
                cs64, b4 = sincos(ph, li64, 48, True, bp, "c2")
                mag = T(ph, [128, 48], F32, "mag"); mag64 = T(ph, [128, 48], F32, "mag64")
                fw.op(act, lambda h: h.activation(mag[:], lr[:], AF.Exp), [bp], [bp])
                fw.op(act, lambda h: h.activation(mag64[:], lr[:], AF.Exp, scale=float(TST)), [bp], [bp])
                v1 = T(ph, [128, 48], F32, "v1"); v2 = T(ph, [128, 48], F32, "v2"); lbi = T(ph, [128, 48], F32, "lbi")
                w1 = T(ph, [128, 48], F32, "w1"); w2 = T(ph, [128, 48], F32, "w2")
                fw.op(dve, lambda h: h.tensor_tensor(v1[:], mag[:], cs_[:], ALU.mult), [bp, b2], [bp])
                fw.op(dve, lambda h: h.tensor_tensor(lbi[:], mag[:], sn[:], ALU.mult), [bp, b1], [bp])
                fw.op(dve, lambda h: h.tensor_scalar(v2[:], lbi[:], misc[:, 0:1], None, ALU.mult), [bp, b_id], [bp])
                fw.op(dve, lambda h: h.tensor_tensor(w1[:], mag64[:], cs64[:], ALU.mult), [bp, b4], [bp])
                fw.op(dve, lambda h: h.tensor_tensor(w2[:], mag64[:], sn64[:], ALU.mult), [bp, b3], [bp])
                fw.op(dve, lambda h: h.tensor_scalar(w2[:], w2[:], misc[:, 0:1], None, ALU.mult), [bp, b_id], [bp])
                den = T(ph, [128, 48], F32, "den"); tmp = T(ph, [128, 48], F32, "tmp"); lm1 = T(ph, [128, 48], F32, "lm1")
                cr = T(ph, [128, 48], F32, "cr"); ci = T(ph, [128, 48], F32, "ci")
                fw.op(dve, lambda h: h.tensor_tensor(den[:], LR[:], LR[:], ALU.mult), [bp], [bp])
                fw.op(dve, lambda h: h.tensor_tensor(tmp[:], LI[:], LI[:], ALU.mult), [bp], [bp])
                fw.op(dve, lambda h: h.tensor_tensor(den[:], den[:], tmp[:], ALU.add), [bp], [bp])
                fw.op(dve, lambda h: h.reciprocal(den[:], den[:]), [bp], [bp])
                fw.op(dve, lambda h: h.tensor_scalar(lm1[:], v1[:], -1.0, None, ALU.add), [bp], [bp])
                fw.op(dve, lambda h: h.tensor_tensor(cr[:], lm1[:], LR[:], ALU.mult), [bp], [bp])
                fw.op(dve, lambda h: h.tensor_tensor(tmp[:], lbi[:], LI[:], ALU.mult), [bp], [bp])
                fw.op(dve, lambda h: h.tensor_tensor(cr[:], cr[:], tmp[:], ALU.add), [bp], [bp])
                fw.op(dve, lambda h: h.tensor_tensor(cr[:], cr[:], den[:], ALU.mult), [bp], [bp])
                fw.op(dve, lambda h: h.tensor_tensor(ci[:], lbi[:], LR[:], ALU.mult), [bp], [bp])
                fw.op(dve, lambda h: h.tensor_tensor(tmp[:], lm1[:], LI[:], ALU.mult), [bp], [bp])
                fw.op(dve, lambda h: h.tensor_tensor(ci[:], ci[:], tmp[:], ALU.subtract), [bp], [bp])
                fw.op(dve, lambda h: h.tensor_tensor(ci[:], ci[:], den[:], ALU.mult), [bp], [bp])
                fw.op(dve, lambda h: h.tensor_scalar(ci[:], ci[:], misc[:, 1:2], None, ALU.mult), [bp, b_id], [bp])
                X1 = T(ph, [128, 48, 16], F32, "X1"); X2 = T(ph, [128, 48, 16], F32, "X2")
                fw.dma(sp, X1[0:64, :, :], b_re[l].rearrange("d g n q -> n (d g) q"), writes=[bp])
                fw.dma(sp, X1[64:128, :, :], b_im[l].rearrange("d g n q -> n (d g) q"), writes=[bp])
                fw.dma(sp, X2[0:64, :, :], b_im[l].rearrange("d g n q -> n (d g) q"), writes=[bp])
                fw.dma(sp, X2[64:128, :, :], b_re[l].rearrange("d g n q -> n (d g) q"), writes=[bp])
                fw.op(dve, lambda h: h.tensor_tensor(X1[:], X1[:], cr[:].unsqueeze(2).to_broadcast([128, 48, 16]), ALU.mult), [bp], [bp])
                fw.op(dve, lambda h: h.tensor_tensor(X2[:], X2[:], ci[:].unsqueeze(2).to_broadcast([128, 48, 16]), ALU.mult), [bp], [bp])
                fw.op(dve, lambda h: h.tensor_tensor(X1[:], X1[:], X2[:], ALU.add), [bp], [bp])
                CN = T(ph, [128, 6, 2, 64], F32, "CN")
                fw.dma(sp, CN[:, :, 0, :], c_re[l].rearrange("d (j e) p n -> (e p) (d j) n", e=8), writes=[bp])
                fw.dma(sp, CN[:, :, 1, :], c_im[l].rearrange("d (j e) p n -> (e p) (d j) n", e=8), writes=[bp])
                fw.op(dve, lambda h: h.tensor_scalar(CN[:, :, 1, :], CN[:, :, 1, :], -1.0, None, ALU.mult), [bp], [bp])
                dcol = T(ph, [128, 3], F32, "dcol")
                for j in range(3):
                    fw.dma(sp, dcol[:, j:j + 1], s5_d[l, j * 128:(j + 1) * 128].rearrange("(p o) -> p o", o=1), writes=[bp])
                wg = T(ph, [128, 3, 2 * S5W], BF16, "wg"); bwg = Buf()
                for c in range(3):
                    load_w(wg[:, c, :], w_glu[l, c * 128:(c + 1) * 128, :], bwg, 2 * S5W)

                if S_STOP == 1:
                    fw.barrier()
                    return
                GW = min(1024, L)
                ge1 = T(ph, [128, GW], F32, "ge1"); ge2 = T(ph, [128, GW], F32, "ge2"); bge1 = Buf(); bge2 = Buf()
                ygo = [T(ph, [128, GW], BF16, "ygo") for _ in range(2)]; bygo = [Buf(), Buf()]
                dummy_w = T(ph, [128, 32], BF16, "dummy_w"); bdw = Buf()
                fw.op(pool, lambda h: h.memset(dummy_w[:], 0.0), [], [bdw])
                A = T(ph, [128, 16, 128], F32, "A"); A64 = T(ph, [128, 16, 128], F32, "A64")
                Bm = T(ph, [128, 16, 128], BF16, "Bm"); Cm = T(ph, [128, 16, 128], F32, "Cm")
                bA = Buf(); bA64 = Buf(); bBm = Buf(); bCm = Buf()
                U = T(ph, [128, NS, L], BF16, "U"); bU = Buf()
                Y = T(ph, [128, NS, L], F32, "Y"); bY = Buf()
                H = T(ph, [128, 16, NC2], F32, "H"); bH = [Buf() for _ in range(4)]
                IA = T(ph, [128, 16, NC2], F32, "IA"); bIA = [Buf() for _ in range(4)]
                pr = [P(ph, [128, 512], F32, "pr") for _ in range(7)]
                bpr = [PB() for _ in range(7)]
                ptr = pr[6]; bptr = bpr[6]
                evi = [0]
                GB = 4
                assert GB * NC2 <= 512

                def ev_eng():
                    evi[0] += 1
                    return act if evi[0] % 2 else dve

                for j in range(3):
                    for d in range(2):
                        for e in range(8):
                            gd = d * 8 + e
                            col = d * 24 + j * 8 + e
                            fw.op(dve, lambda h: h.tensor_scalar(A[:, gd, :], identf[:], v1[:, col:col + 1], None, ALU.mult), [bp, b_id], [bA])
                            fw.op(dve, lambda h: h.scalar_tensor_tensor(A[:, gd, :], jswap[:], v2[:, col:col + 1], A[:, gd, :], ALU.mult, ALU.add), [bp, b_id, bA], [bA])
                            fw.op(dve, lambda h: h.tensor_scalar(A64[:, gd, :], identf[:], w1[:, col:col + 1], None, ALU.mult), [bp, b_id], [bA64])
                            fw.op(dve, lambda h: h.scalar_tensor_tensor(A64[:, gd, :], jswap[:], w2[:, col:col + 1], A64[:, gd, :], ALU.mult, ALU.add), [bp, b_id, bA64], [bA64])
                        c0 = d * 24 + j * 8
                        fw.op(pe, lambda h: h.transpose(ptr[:, 0:128], X1[:, c0:c0 + 8, :].rearrange("p g q -> p (g q)"), identf[:]), [bp, b_id], [bptr])
                        for e in range(8):
                            gd = d * 8 + e
                            fw.op(dve, lambda h: h.tensor_scalar(Bm[:, gd, :], ptr[:, 0:128], misc[:, 8 + e:9 + e], None, ALU.mult), [bptr, b_id], [bBm])
                        fw.op(pe, lambda h: h.transpose(ptr[:, 128:256], CN[:, d * 3 + j, :, :].rearrange("p a n -> p (a n)"), identf[:]), [bp, b_id], [bptr])
                        fw.op(pool, lambda h: h.memset(Cm[:, d * 8:(d + 1) * 8, :], 0.0), [], [bCm])
                        for e in range(8):
                            gd = d * 8 + e
                            fw.op(dve, lambda h: h.tensor_copy(Cm[:, gd, e * 16:(e + 1) * 16], ptr[:, 128 + e * 16:128 + (e + 1) * 16]), [bptr], [bCm])
                    if S_STOP == 2:
                        fw.barrier()
                        return
                    for s in range(NS):
                        fw.dma(sp, U[:, s, :], zs5T_d[s][j * 128:(j + 1) * 128, :], reads=[b_zs5T[s]], writes=[bU])
                    fw.op(dve, lambda h: h.tensor_scalar(Y[:], U[:], dcol[:, j:j + 1], None, ALU.mult), [bU, bp], [bY])

                    def ucols(d, tau):
                        pos = tau if d == 0 else TST - 1 - tau
                        return U[:, :, pos:L:TST]

                    def step(tau, X, bX, first, with_y):
                        for q in range(4):
                            bank, bb = pr[q], bpr[q]
                            d = q // 2
                            for i in range(GB):
                                gd = q * GB + i
                                reg = bank[:, i * NC2:(i + 1) * NC2]
                                fw.op(pe, lambda h: h.matmul(reg, Bm[:, gd, :], ucols(d, tau), start=(i == 0), stop=first, skip_group_check=True), [bBm, bU], [bb])
                            if not first:
                                for i in range(GB):
                                    gd = q * GB + i
                                    reg = bank[:, i * NC2:(i + 1) * NC2]
                                    fw.op(pe, lambda h: h.matmul(reg, A[:, gd, :], X[:, gd, :], start=False, stop=True, skip_group_check=True), [bA, bX[q]], [bb])
                            evac(ev_eng(), X[:, q * GB:(q + 1) * GB, :], bank[:, 0:GB * NC2].rearrange("p (g c) -> p g c", g=GB), [bb], [bX[q]])
                            if with_y and q % 2 == 1:
                                yb_, byb_ = pr[4 + d], bpr[4 + d]
                                for e in range(8):
                                    gd = d * 8 + e
                                    fw.op(pe, lambda h: h.matmul(yb_[:, 0:NC2], Cm[:, gd, :], X[:, gd, :], start=(e == 0), stop=(e == 7)), [bCm, bX[gd // GB]], [byb_])
                                pos = tau if d == 0 else TST - 1 - tau
                                yv = Y[:, :, pos:L:TST]
                                fw.op(dve, lambda h: h.tensor_tensor(yv, yv, yb_[:, 0:NC2].rearrange("p (s c) -> p s c", s=NS), ALU.add), [byb_, bY], [bY])

                    for tau in range(TST):
                        step(tau, H, bH, tau == 0, False)
                    if S_STOP == 3:
                        fw.barrier()
                        return
                    Hv = H[:].rearrange("p g (s c) -> p g s c", s=NS)
                    IAv = IA[:].rearrange("p g (s c) -> p g s c", s=NS)
                    for q in range(4):
                        d = q // 2
                        c_first = 0 if d == 0 else NC_ - 1
                        fw.op(pool, lambda h: h.memset(IAv[:, q * GB:(q + 1) * GB, :, c_first:c_first + 1], 0.0), [], [bIA[q]])
                    for cstep in range(NC_ - 1):
                        for q in range(4):
                            d = q // 2
                            c = cstep if d == 0 else NC_ - 1 - cstep
                            cn_ = c + 1 if d == 0 else c - 1
                            bank, bb = pr[q], bpr[q]
                            for i in range(GB):
                                gd = q * GB + i
                                fw.op(pe, lambda h: h.matmul(bank[:, i * NS:(i + 1) * NS], A64[:, gd, :], IAv[:, gd, :, c], start=(i == 0), stop=True, skip_group_check=True), [bA64, bIA[q]], [bb])
                            fw.op(dve, lambda h: h.tensor_tensor(IAv[:, q * GB:(q + 1) * GB, :, cn_], bank[:, 0:GB * NS].rearrange("p (g s) -> p g s", g=GB),
                                                                 Hv[:, q * GB:(q + 1) * GB, :, c], ALU.add), [bb, bH[q]], [bIA[q]])
                    if S_STOP == 4:
                        fw.barrier()
                        return
                    for tau in range(TST):
                        step(tau, IA, bIA, False, True)
                    if S_STOP == 5:
                        fw.barrier()
                        return
                    kg = 0
                    for s in range(NS):
                        for hb_ in range(L // GW):
                            wd = GW
                            sl = slice(hb_ * wd, (hb_ + 1) * wd)
                            yo = ygo[kg % 2]; byo = bygo[kg % 2]
                            kg += 1
                            g1 = ge1[:, 0:wd]; g2 = ge2[:, 0:wd]
                            yv = Y[:, s, sl]
                            fw.op(dve, lambda h: h.tensor_tensor(g1, yv, yv, ALU.mult), [bY], [bge1])
                            fw.op(dve, lambda h: h.tensor_scalar(g1, g1, 0.044715, 1.0, ALU.mult, ALU.add), [bge1], [bge1])
                            fw.op(pool, lambda h: h.tensor_tensor(g1, g1, yv, ALU.mult), [bge1, bY], [bge1])
                            fw.op(act, lambda h: h.activation(g2, g1, AF.Sigmoid, scale=1.5957691216057308), [bge1], [bge2])
                            fw.op(dve, lambda h: h.tensor_tensor(yo[:], yv, g2, ALU.mult), [bY, bge2], [byo])
                            fw.dma(pool, ygT_d[s][j * 128:(j + 1) * 128, sl], yo[:], reads=[byo], writes=[b_ygT[s]])
                fw.barrier()
                fw.op(pe, lambda h: h.matmul(pr[5][0:32, 0:32], dummy_w[:], dummy_w[:], start=True, stop=True), [bdw], [bptr])
                fw.barrier()
                gl_t = [T(ph, [128, 3, 512], BF16, "gl_t") for _ in range(2)]; bgl = [Buf(), Buf()]
                yin = [T(ph, [128, 3, 512], BF16, "yin") for _ in range(2)]; byin = [Buf(), Buf()]
                sg = T(ph, [128, 512], F32, "sg"); bsg = Buf()
                pg = [(pr[i], bpr[i]) for i in range(4)]
                k = 0
                for s in range(NS):
                    for tb in range(NB):
                        gt = gl_t[k % 2]; bgt = bgl[k % 2]
                        yi_ = yin[k % 2]; byi_ = byin[k % 2]
                        k += 1
                        fw.dma(sp, yi_[:], ygT_d[s][:, tb * 512:(tb + 1) * 512].rearrange("(m p) t -> p m t", p=128), reads=[b_ygT[s]], writes=[byi_])
                        for m in range(3):
                            pa_, bpa_ = pg[(2 * m) % 4]
                            pg_, bpg_ = pg[(2 * m + 1) % 4]
                            for c in range(3):
                                fw.op(pe, lambda h: h.matmul(pa_[:], wg[:, c, m * 128:(m + 1) * 128], yi_[:, c, :], start=(c == 0), stop=(c == 2)), [bwg, byi_], [bpa_])
                            for c in range(3):
                                fw.op(pe, lambda h: h.matmul(pg_[:], wg[:, c, S5W + m * 128:S5W + (m + 1) * 128], yi_[:, c, :], start=(c == 0), stop=(c == 2)), [bwg, byi_], [bpg_])
                            fw.op(act, lambda h: h.activation(sg[:], pg_[:], AF.Sigmoid), [bpg_], [bsg])
                            fw.op(dve, lambda h: h.tensor_tensor(gt[:, m, :], pa_[:], sg[:], ALU.mult), [bpa_, bsg], [bgt])
                        fw.dma(pool, gluT_d[s][:, tb * 512:(tb + 1) * 512].rearrange("(m p) t -> p m t", p=128), gt[:], reads=[bgt], writes=[b_gluT[s]])
                fw.barrier()

        def phase_M(l, s):
            with ExitStack() as ph:
                cq = T(ph, [128, 3, L], BF16, "cq"); ckv = T(ph, [128, 2, L], BF16, "ckv"); bc = Buf()
                fw.dma(sp, cq[:], cqnT_d[s].rearrange("(m p) t -> p m t", p=128), reads=[b_cqnT[s]], writes=[bc])
                fw.dma(sp, ckv[:], ckvnT_d[s].rearrange("(m p) t -> p m t", p=128), reads=[b_ckvnT[s]], writes=[bc])
                kpe = T(ph, [96, L], BF16, "kpe")
                fw.dma(sp, kpe[64:96, :], kpeT_d[s], reads=[b_kpeT[s]], writes=[bc])
                rct = T(ph, [96, 2, L], F32, "rct")
                fw.dma(sp, rct[64:96, :, :], c_rope.rearrange("a d t -> d a t"), writes=[bc])
                wq = T(ph, [128, 3, NH * 96], BF16, "wq"); wqs = T(ph, [128, 3, NH * 96], BF16, "wqs"); wkv = T(ph, [128, 2, NH * 128], BF16, "wkv")
                bwq = Buf(); bwk = Buf()
                for c in range(3):
                    load_w(wq[:, c, :], w_qb[l, c * 128:(c + 1) * 128, :], bwq, NH * 96)
                for c in range(2):
                    load_w(wkv[:, c, :], w_kvb[l, c * 128:(c + 1) * 128, :], bwk, NH * 128)
                for c in range(3):
                    wq4 = wq[:, c, :].rearrange("p (h e) -> p h e", h=NH)
                    wqs4 = wqs[:, c, :].rearrange("p (h e) -> p h e", h=NH)
                    fw.op(pool, lambda h: h.tensor_copy(wqs4[:, :, 0:64], wq4[:, :, 0:64]), [bwq], [bwq])
                    fw.op(pool, lambda h: h.tensor_copy(wqs4[:, :, 64:80], wq4[:, :, 80:96]), [bwq], [bwq])
                    fw.op(pool, lambda h: h.tensor_copy(wqs4[:, :, 80:96], wq4[:, :, 64:80]), [bwq], [bwq])
                QT = [T(ph, [96, L], BF16, "QT") for _ in range(2)]; bQT = [Buf(), Buf()]
                KT = [T(ph, [96, L], BF16, "KT") for _ in range(2)]; bKT = [Buf(), Buf()]
                V = [T(ph, [128, NTC, 65], BF16, "V") for _ in range(2)]; bV = [Buf(), Buf()]
                for i in range(2):
                    fw.op(pool, lambda h: h.memset(V[i][:, :, 64:65], 1.0), [], [bV[i]])
                t1 = T(ph, [96, 512], F32, "t1"); t2 = T(ph, [96, 512], F32, "t2"); bt1 = Buf(); bt2 = Buf()
                KP = 2 if NTC % 2 == 0 else 1
                pT_ = [T(ph, [128, KP, 512], BF16, "pT") for _ in range(3)]; bpT_ = [Buf() for _ in range(3)]
                o_t = [T(ph, [128, 4, 64], BF16, "o_t") for _ in range(2)]; bo_t = [Buf(), Buf()]
                oTs = [T(ph, [65, 512], F32, "oTs") for _ in range(2)]; boTs = [Buf(), Buf()]
                rec = T(ph, [128, 4], F32, "rec"); brec = Buf()
                psS = [P(ph, [128, KP, 512], F32, "psS") for _ in range(2)]; bpsS = [PB() for _ in range(2)]
                psO = [P(ph, [128, 512], F32, "psO") for _ in range(2)]; bpsO = [PB(), PB()]
                psP = [P(ph, [128, 512], F32, "psP") for _ in range(2)]; bpsP = [PB() for _ in range(2)]
                ppi = [0]; si = [0]; oi = [0]; evi = [0]

                def npp():
                    i = ppi[0] % 2
                    ppi[0] += 1
                    return psP[i], bpsP[i]

                def ev_eng():
                    return dve

                def prep_head(hh):
                    i = hh % 2
                    qt, bqt, kt, bkt, v, bv = QT[i], bQT[i], KT[i], bKT[i], V[i], bV[i]
                    for tb in range(NB):
                        sl = slice(tb * 512, (tb + 1) * 512)
                        ps, bps = npp()
                        ps2, bps2 = npp()
                        for c in range(3):
                            fw.op(pe, lambda h: h.matmul(ps[0:96, :], wq[:, c, hh * 96:(hh + 1) * 96], cq[:, c, sl], start=(c == 0), stop=(c == 2)), [bwq, bc], [bps])
                        for c in range(3):
                            fw.op(pe, lambda h: h.matmul(ps2[0:96, :], wqs[:, c, hh * 96:(hh + 1) * 96], cq[:, c, sl], start=(c == 0), stop=(c == 2)), [bwq, bc], [bps2])
                        fw.op(dve, lambda h: h.tensor_scalar(qt[0:64, sl], ps[0:64, :], SCALE, None, ALU.mult), [bps], [bqt])
                        fw.op(dve, lambda h: h.scalar_tensor_tensor(t1[64:96, :], ps[64:96, :], SCALE, rct[64:96, 0, sl], ALU.mult, ALU.mult), [bps, bc], [bt1])
                        fw.op(dve, lambda h: h.scalar_tensor_tensor(t2[64:96, :], ps2[64:96, :], SCALE, rct[64:96, 1, sl], ALU.mult, ALU.mult), [bps2, bc], [bt2])
                        fw.op(pool, lambda h: h.tensor_tensor(qt[64:96, sl], t1[64:96, :], t2[64:96, :], ALU.add), [bt1, bt2], [bqt])
                        ps3, bps3 = npp()
                        for c in range(2):
                            fw.op(pe, lambda h: h.matmul(ps3[0:64, :], wkv[:, c, hh * 128:hh * 128 + 64], ckv[:, c, sl], start=(c == 0), stop=(c == 1)), [bwk, bc], [bps3])
                        evac(ev_eng(), kt[0:64, sl], ps3[0:64, :], [bps3], [bkt])
                        yield
                    fw.op(pool, lambda h: h.tensor_copy(kt[64:96, :], kpe[64:96, :]), [bc], [bkt])
                    for t8 in range(NTC // 8 if NTC >= 8 else 1):
                        nt8 = min(8, NTC)
                        ps, bps = npp()
                        for tt in range(nt8):
                            tcx = t8 * 8 + tt
                            for c in range(2):
                                fw.op(pe, lambda h: h.matmul(ps[:, tt * 64:(tt + 1) * 64], ckv[:, c, tcx * 128:(tcx + 1) * 128], wkv[:, c, hh * 128 + 64:hh * 128 + 128], start=(c == 0 and tt == 0), stop=(c == 1), skip_group_check=True), [bwk, bc], [bps])
                        evac(ev_eng(), v[:, t8 * 8:t8 * 8 + nt8, 0:64], ps[:, 0:nt8 * 64].rearrange("p (t e) -> p t e", e=64), [bps], [bv])
                        yield

                NK = NTC // KP
                for _ in prep_head(0):
                    pass
                for hh in range(NH):
                    prep = prep_head(hh + 1) if hh + 1 < NH else iter(())
                    i = hh % 2
                    qt, bqt, kt, bkt, v, bv = QT[i], bQT[i], KT[i], bKT[i], V[i], bV[i]
                    its = [(qb, kc2) for qb in range(NB) for kc2 in range(NK)]

                    def emit_S(n):
                        qb, kc2 = its[n]
                        pss, bpss = psS[n % 2], bpsS[n % 2]
                        for u in range(KP):
                            kc = kc2 * KP + u
                            fw.op(pe, lambda h: h.matmul(pss[:, u, :], kt[:, kc * 128:(kc + 1) * 128], qt[:, qb * 512:(qb + 1) * 512], start=True, stop=True), [bkt, bqt], [bpss])

                    def epilogue_rest(qb, ots, bots, ot, bot):
                        ptr_, bptr_ = npp()
                        for qs in range(4):
                            fw.op(pe, lambda h: h.transpose(ptr_[:, qs * 65:(qs + 1) * 65], ots[:, qs * 128:(qs + 1) * 128], identf[0:65, 0:65]), [bots, b_id], [bptr_])
                        pv_ = ptr_[:, 0:260].rearrange("p (q e) -> p q e", e=65)
                        fw.op(dve, lambda h: h.reciprocal(rec[:], pv_[:, :, 64]), [bptr_], [brec])
                        fw.op(dve, lambda h: h.tensor_tensor(ot[:], pv_[:, :, 0:64], rec[:].unsqueeze(2).to_broadcast([128, 4, 64]), ALU.mult), [bptr_, brec], [bot])
                        fw.dma(pool, o_d[s][qb * 512:(qb + 1) * 512, hh * 64:(hh + 1) * 64].rearrange("(q p) e -> p q e", p=128), ot[:], reads=[bot], writes=[b_o[s]])

                    pending = []
                    emit_S(0)
                    pstep = max(1, len(its) // 14)
                    for n, (qb, kc2) in enumerate(its):
                        if n + 1 < len(its):
                            emit_S(n + 1)
                        if n % pstep == pstep - 1:
                            next(prep, None)
                        po, bpo = psO[(oi[0] + qb) % 2], bpsO[(oi[0] + qb) % 2]
                        pss, bpss = psS[n % 2], bpsS[n % 2]
                        pt, bpt = pT_[n % 3], bpT_[n % 3]
                        fw.op(act, lambda h: h.activation(pt[:], pss[:], AF.Exp), [bpss], [bpt])
                        if pending and pending[0][0] <= n:
                            epilogue_rest(*pending.pop(0)[1])
                        for u in range(KP):
                            kc = kc2 * KP + u
                            fw.op(pe, lambda h: h.matmul(po[0:65, :], v[:, kc, :], pt[:, u, :], start=(kc == 0), stop=(kc == NTC - 1)), [bpt, bv], [bpo])
                        if kc2 == NK - 1:
                            k_ = (oi[0] + qb) % 2
                            ots, bots = oTs[k_], boTs[k_]
                            ot, bot = o_t[k_], bo_t[k_]
                            fw.op(dve, lambda h: h.tensor_copy(ots[:], po[0:65, :]), [bpo], [bots])
                            pending.append((n + 2, (qb, ots, bots, ot, bot)))
                    while pending:
                        epilogue_rest(*pending.pop(0)[1])
                    for _ in prep:
                        pass
                    oi[0] += NB
                fw.barrier()

        def phase_C1(l, seqs, rider=iter(())):
            with ExitStack() as ph:
                wf = T(ph, [128, 3, D], BF16, "wf"); ws = T(ph, [128, 3, D], BF16, "ws")
                wo = T(ph, [128, 8, D], BF16, "wo"); wout = T(ph, [128, 8, D], BF16, "wout")
                bw = Buf()
                for c in range(3):
                    load_w(wf[:, c, :], w_fnet[l, c * 128:(c + 1) * 128, :], bw, D)
                    load_w(ws[:, c, :], w_s5[l, c * 128:(c + 1) * 128, :], bw, D)
                for c in range(8):
                    load_w(wo[:, c, :], w_o[l, c * 128:(c + 1) * 128, :], bw, D)
                    load_w(wout[:, c, :], w_out[l, c * 128:(c + 1) * 128, :], bw, D)
                NBUF = 3
                fm = [T(ph, [128, 3, 128], BF16, "fm") for _ in range(NBUF)]
                gl = [T(ph, [128, 3, 128], BF16, "gl") for _ in range(NBUF)]
                ot = [T(ph, [128, D], BF16, "ot") for _ in range(NBUF)]
                gt = [T(ph, [128, 3 * D], BF16, "gt") for _ in range(NBUF)]
                xt = [T(ph, [128, D], F32, "xt") for _ in range(NBUF)]
                bin_ = [Buf() for _ in range(NBUF)]
                oT = T(ph, [128, 8, 128], BF16, "oT"); boT = Buf()
                m1 = [T(ph, [128, D], F32, "m1") for _ in range(2)]; m2 = [T(ph, [128, D], F32, "m2") for _ in range(2)]
                m3 = [T(ph, [128, D], F32, "m3") for _ in range(2)]
                bm1 = [Buf(), Buf()]; bm2 = [Buf(), Buf()]; bm3 = [Buf(), Buf()]
                mb = [T(ph, [128, D], BF16, "mb") for _ in range(2)]; bmb = [Buf(), Buf()]
                mT = T(ph, [128, 8, 128], BF16, "mT"); bmT = Buf()
                xo = [T(ph, [128, D], F32, "xo") for _ in range(2)]; bxo = [Buf(), Buf()]
                pT = [P(ph, [128, 1024], BF16, "pT") for _ in range(2)]; bpT = [PB(), PB()]
                pa = [P(ph, [128, 512], F32, "pa") for _ in range(6)]; bpa = [PB() for _ in range(6)]
                pai = [0]; pti = [0]

                def nb():
                    i = pai[0] % 6
                    pai[0] += 1
                    return pa[i], bpa[i]

                def nt():
                    i = pti[0] % 2
                    pti[0] += 1
                    return pT[i], bpT[i]

                def load(i):
                    k = i % NBUF
                    sl = slice(i * 128, (i + 1) * 128)
                    fw.dma(sp, fm[k][:], fmT_d[s][:, sl].rearrange("(m p) t -> p m t", p=128), reads=[b_fmT[s]], writes=[bin_[k]])
                    fw.dma(sp, gl[k][:], gluT_d[s][:, sl].rearrange("(m p) t -> p m t", p=128), reads=[b_gluT[s]], writes=[bin_[k]])
                    fw.dma(sp, ot[k][:], o_d[s][sl, :], reads=[b_o[s]], writes=[bin_[k]])
                    fw.dma(sp, gt[k][:], gates_d[s][sl, :], reads=[b_gates[s]], writes=[bin_[k]])
                    fw.dma(sp, xt[k][:], xsrc[sl, :], reads=[b_xsrc], writes=[bin_[k]])

                def part1(i):
                    k = i % NBUF
                    k2 = i % 2
                    bi = bin_[k]
                    pt, bpt = nt()
                    for c in range(8):
                        fw.op(pe, lambda h: h.transpose(pt[:, c * 128:(c + 1) * 128], ot[k][:, c * 128:(c + 1) * 128], identb[:]), [bi, b_id], [bpt])
                    evac(act, oT[:].rearrange("p c t -> p (c t)"), pt[:], [bpt], [boT])
                    for cb in range(2):
                        cs = slice(cb * 512, (cb + 1) * 512)
                        pya, bpya = nb()
                        for c in range(3):
                            fw.op(pe, lambda h: h.matmul(pya[:], fm[k][:, c, :], wf[:, c, cs], start=(c == 0), stop=(c == 2)), [bi, bw], [bpya])
                        pyb, bpyb = nb()
                        for c in range(3):
                            fw.op(pe, lambda h: h.matmul(pyb[:], gl[k][:, c, :], ws[:, c, cs], start=(c == 0), stop=(c == 2)), [bi, bw], [bpyb])
                        pyc, bpyc = nb()
                        for c in range(8):
                            fw.op(pe, lambda h: h.matmul(pyc[:], oT[:, c, :], wo[:, c, cs], start=(c == 0), stop=(c == 7)), [boT, bw], [bpyc])
                        fw.op(dve, lambda h: h.tensor_tensor(m1[k2][:, cs], pya[:], gt[k][:, cb * 512:(cb + 1) * 512], ALU.mult), [bpya, bi], [bm1[k2]])
                        fw.op(dve, lambda h: h.tensor_tensor(m2[k2][:, cs], pyb[:], gt[k][:, D + cb * 512:D + (cb + 1) * 512], ALU.mult), [bpyb, bi], [bm2[k2]])
                        fw.op(dve, lambda h: h.tensor_tensor(m3[k2][:, cs], pyc[:], gt[k][:, 2 * D + cb * 512:2 * D + (cb + 1) * 512], ALU.mult), [bpyc, bi], [bm3[k2]])
                    fw.op(pool, lambda h: h.tensor_tensor(m1[k2][:], m1[k2][:], m2[k2][:], ALU.add), [bm1[k2], bm2[k2]], [bm1[k2]])
                    fw.op(pool, lambda h: h.tensor_tensor(mb[k2][:], m1[k2][:], m3[k2][:], ALU.add), [bm1[k2], bm3[k2]], [bmb[k2]])

                def part2(i):
                    k = i % NBUF
                    k2 = i % 2
                    bi = bin_[k]
                    pt, bpt = nt()
                    for c in range(8):
                        fw.op(pe, lambda h: h.transpose(pt[:, c * 128:(c + 1) * 128], mb[k2][:, c * 128:(c + 1) * 128], identb[:]), [bmb[k2], b_id], [bpt])
                    evac(act, mT[:].rearrange("p c t -> p (c t)"), pt[:], [bpt], [bmT])
                    for cb in range(2):
                        cs = slice(cb * 512, (cb + 1) * 512)
                        po, bpo = nb()
                        for c in range(8):
                            fw.op(pe, lambda h: h.matmul(po[:], mT[:, c, :], wout[:, c, cs], start=(c == 0), stop=(c == 7)), [bmT, bw], [bpo])
                        fw.op(dve, lambda h: h.tensor_tensor(xo[k2][:, cs], po[:], xt[k][:, cs], ALU.add), [bpo, bi], [bxo[k2]])
                    fw.dma(pool, xdst[i * 128:(i + 1) * 128, :], xo[k2][:], reads=[bxo[k2]], writes=[b_xdst])

                for (s, xsrc, b_xsrc, xdst, b_xdst) in seqs:
                    load(0)
                    if NTC > 1:
                        load(1)
                    part1(0)
                    for i in range(NTC):
                        if i + 2 < NTC:
                            load(i + 2)
                        if i + 1 < NTC:
                            part1(i + 1)
                        part2(i)
                        next(rider, None)
                for _ in rider:
                    pass
                fw.barrier()

        def phase_C2(l, seqs, wu, bwu):
            TB = 256
            with ExitStack() as ph:
                wd = T(ph, [128, 32, D], BF16, "wd"); bw = Buf()
                for c in range(32):
                    load_w(wd[:, c, :], w_down[l, c * 128:(c + 1) * 128, :], bw, D)
                gm = T(ph, [128, D], F32, "gm"); gf = T(ph, [128, D], F32, "gf"); bg = Buf()
                fw.dma(sp, gm[:], g_mlp[l].partition_broadcast(128), writes=[bg])
                fw.dma(sp, gf[:], g_final.partition_broadcast(128), writes=[bg])
                xt = [T(ph, [128, 2, D], F32, "xt") for _ in range(2)]; bxt = [Buf(), Buf()]
                hb = T(ph, [128, D], BF16, "hb"); bhb = Buf()
                junk = hb; bjunk = bhb
                ss = T(ph, [128, 2], F32, "ss"); bss = Buf()
                ss3 = T(ph, [128, 1], F32, "ss3"); bss3 = Buf()
                hTs = [T(ph, [128, 8, TB], BF16, "hT") for _ in range(2)]; bhTs = [Buf(), Buf()]
                rl = T(ph, [128, 2, TB], F32, "rl"); brl = Buf()
                aT = T(ph, [128, 32, TB], BF16, "aT"); baT = Buf()
                xo = [T(ph, [128, D], F32, "xo") for _ in range(2)]; bxo = [Buf(), Buf()]
                pT = [P(ph, [128, 1024], BF16, "pT") for _ in range(2)]; bpT = [PB(), PB()]
                pa = [P(ph, [128, 512], F32, "pa") for _ in range(6)]; bpa = [PB() for _ in range(6)]
                pai = [0]; pti = [0]; xoi = [0]

                def nb():
                    i = pai[0] % 6
                    pai[0] += 1
                    return pa[i], bpa[i]

                def nt():
                    i = pti[0] % 2
                    pti[0] += 1
                    return pT[i], bpT[i]

                nblk = L // TB

                def load(b):
                    fw.dma(sp, xt[b % 2][:], xsrc[b * TB:(b + 1) * TB, :].rearrange("(j p) d -> p j d", p=128), reads=[b_xsrc], writes=[bxt[b % 2]])

                def pre(b):
                    xi = xt[b % 2]; bxi = bxt[b % 2]
                    hT = hTs[b % 2]; bhT = bhTs[b % 2]
                    for j in range(2):
                        fw.op(act, lambda h: h.activation(junk[:], xi[:, j, :], AF.Square, accum_out=ss[:, j:j + 1]), [bxi], [bjunk, bss])
                    rstd_from_ss(None, ss, 2, D, bss, bss)
                    for j in range(2):
                        fw.op(dve, lambda h: h.scalar_tensor_tensor(hb[:], xi[:, j, :], ss[:, j:j + 1], gm[:], ALU.mult, ALU.mult), [bxi, bss, bg], [bhb])
                        pt, bpt = nt()
                        for c in range(8):
                            fw.op(pe, lambda h: h.transpose(pt[:, c * 128:(c + 1) * 128], hb[:, c * 128:(c + 1) * 128], identb[:]), [bhb, b_id], [bpt])
                        evac(act if j else dve, hT[:, :, j * 128:(j + 1) * 128], pt[:].rearrange("p (c t) -> p c t", c=8), [bpt], [bhT])

                def up(b):
                    hT = hTs[b % 2]; bhT = bhTs[b % 2]
                    for f2 in range(16):
                        ps, bps = nb()
                        for ff in range(2):
                            f = f2 * 2 + ff
                            for c in range(8):
                                fw.op(pe, lambda h: h.matmul(ps[:, ff * TB:(ff + 1) * TB], wu[:, c, f * 128:(f + 1) * 128], hT[:, c, :], start=(c == 0 and ff == 0), stop=(c == 7), skip_group_check=True), [bwu, bhT], [bps])
                        fw.op(act, lambda h: h.activation(rl[:].rearrange("p a t -> p (a t)"), ps[:], AF.Relu), [bps], [brl])
                        eng = pool if f2 % 2 else dve
                        fw.op(eng, lambda h: h.tensor_tensor(aT[:, f2 * 2:f2 * 2 + 2, :], rl[:], rl[:], ALU.mult), [brl], [baT])

                def down(b):
                    xi = xt[b % 2]; bxi = bxt[b % 2]
                    for j in range(2):
                        xk = xo[xoi[0] % 2]; bxk = bxo[xoi[0] % 2]
                        xoi[0] += 1
                        for cb in range(2):
                            cs = slice(cb * 512, (cb + 1) * 512)
                            ps, bps = nb()
                            for f in range(32):
                                fw.op(pe, lambda h: h.matmul(ps[:], aT[:, f, j * 128:(j + 1) * 128], wd[:, f, cs], start=(f == 0), stop=(f == 31)), [baT, bw], [bps])
                            fw.op(dve, lambda h: h.tensor_tensor(xk[:, cs], ps[:], xi[:, j, cs], ALU.add), [bps, bxi], [bxk])
                        if final:
                            fw.op(act, lambda h: h.activation(rl[:].rearrange("p a t -> p (a t)"), xk[:, 0:2 * TB], AF.Square, accum_out=ss3[:, 0:1]), [bxk], [brl, bss3])
                            fw.op(act, lambda h: h.activation(rl[:].rearrange("p a t -> p (a t)"), xk[:, 2 * TB:4 * TB], AF.Square, accum_out=ss[:, 0:1]), [bxk], [brl, bss])
                            fw.op(dve, lambda h: h.tensor_tensor(ss3[:], ss3[:], ss[:, 0:1], ALU.add), [bss3, bss], [bss3])
                            rstd_from_ss(None, ss3, 1, D, bss3, bss3)
                            fw.op(dve, lambda h: h.scalar_tensor_tensor(xk[:], xk[:], ss3[:, 0:1], gf[:], ALU.mult, ALU.mult), [bxk, bss3, bg], [bxk])
                        r0 = b * TB + j * 128
                        fw.dma(pool, xdst[r0:r0 + 128, :], xk[:], reads=[bxk], writes=[b_xdst])

                for (s, xsrc, b_xsrc, xdst, b_xdst, final) in seqs:
                    load(0)
                    pre(0)
                    for b in range(nblk):
                        if b + 1 < nblk:
                            load(b + 1)
                        up(b)
                        if b + 1 < nblk:
                            pre(b + 1)
                        down(b)
                fw.barrier()

        fw.barrier()
        done = False
        for l in range(DEPTH):
            last = (l == DEPTH - 1)
            if l == 0:
                phase_A(l, [(s, x_in[s], b_const) for s in range(NS)])
            else:
                phase_A(l, [(s, xb_d[s], b_xb[s]) for s in range(NS)])
            if stop_after == "A":
                break
            for s in range(NS):
                phase_F(l, s)
            if stop_after == "F":
                break
            phase_S(l)
            if stop_after == "S":
                break
            for s in range(NS):
                phase_M(l, s)
            if stop_after == "M":
                break
            with ExitStack() as c12:
                wu_t = T(c12, [128, 8, DFF], BF16, "wu"); bwu_t = Buf()

                def wu_rider():
                    for c in range(8):
                        yield from load_w_gen(wu_t[:, c, :], w_up[l, c * 128:(c + 1) * 128, :], bwu_t, DFF)

                phase_C1(l, [((s, x_in[s], b_const) if l == 0 else (s, xb_d[s], b_xb[s])) + (xa_d[s], b_xa[s]) for s in range(NS)], wu_rider())
                if stop_after == "C1":
                    break
                if last:
                    phase_C2(l, [(s, xa_d[s], b_xa[s], y_out[s], b_y[s], True) for s in range(NS)], wu_t, bwu_t)
                else:
                    phase_C2(l, [(s, xa_d[s], b_xa[s], xb_d[s], b_xb[s], False) for s in range(NS)], wu_t, bwu_t)
        allb = b_y + b_ygT + b_zf + b_zs5T + b_cqnT + b_ckvnT + b_kpeT + b_gates + b_fmT + b_gluT + b_o + b_xa + b_xb
        fw.finish(allb)
        fw.barrier()
        stats = {e.name: (e.nins, e.nwait) for e in fw.engs}
    return nc, stats


def make_consts(L):
    bf = ml_dtypes.bfloat16
    c = {}
    c["c_identf"] = np.eye(128, dtype=np.float32)
    js = np.zeros((128, 128), np.float32)
    for k in range(128):
        js[k, (k + 64) % 128] = 1.0
    c["c_jswap"] = js
    misc = np.zeros((128, 16), np.float32)
    misc[:64, 0] = 1.0
    misc[64:, 0] = -1.0
    misc[:, 1] = -misc[:, 0]
    for e in range(8):
        misc[e * 16:(e + 1) * 16, 8 + e] = 1.0
    c["c_misc"] = misc
    t = np.arange(L, dtype=np.int64)
    tk = (t[:, None] * t[None, :]) % L
    ang = 2.0 * np.pi * tk.astype(np.float64) / L
    dft = np.empty((L, 2, L), dtype=bf)
    dft[:, 0, :] = (np.cos(ang) / math.sqrt(L)).astype(bf)
    dft[:, 1, :] = (np.sin(ang) / math.sqrt(L)).astype(bf)
    c["c_dft"] = dft
    a = np.arange(64)
    ang64 = 2.0 * np.pi * ((a[:, None] * a[None, :]) % 64) / 64.0
    c64 = np.zeros((128, 2, 128), np.float64)
    for blk in range(2):
        sl = slice(blk * 64, (blk + 1) * 64)
        c64[sl, 0, sl] = np.cos(ang64) / 8.0
        c64[sl, 1, sl] = -np.sin(ang64) / 8.0
    c["c_c64"] = c64.astype(bf)
    half = 16
    inv = 10000.0 ** (-np.arange(half, dtype=np.float32) / half)
    angr = np.arange(L, dtype=np.float32)[:, None] * inv[None, :]
    cos = np.cos(angr).T
    sin = np.sin(angr).T
    rope = np.zeros((2, 32, L), np.float32)
    rope[0, :16] = cos
    rope[0, 16:] = cos
    rope[1, :16] = -sin
    rope[1, 16:] = sin
    c["c_rope"] = rope
    return c


WNAMES = ["g_mix", "w_in", "w_fnet", "s5_lam_re", "s5_lam_im", "s5_log_dt", "s5_b_re", "s5_b_im", "s5_c_re",
          "s5_c_im", "s5_d", "w_glu", "w_s5", "g_q", "w_qb", "g_kv", "w_kvb", "w_o_mla", "w_out", "g_mlp",
          "w_up", "w_down", "g_final"]

_CACHE = {}


def kernel(**inputs):
    xp = np.asarray(inputs["x_prompt"], dtype=np.float32)
    xs = np.asarray(inputs["x_sample"], dtype=np.float32)
    L = xp.shape[1]
    DEPTH = inputs["w_in"].shape[0]
    NS = 2
    ncores = 8
    seqs = [xp[i] for i in range(xp.shape[0])] + [xs[i] for i in range(xs.shape[0])]
    nseq = len(seqs)
    assign = [[i, i + ncores] for i in range(ncores)]
    key = (L, NS, DEPTH)
    if key not in _CACHE:
        _CACHE[key] = build_program(L, NS, DEPTH)[0]
    nc = _CACHE[key]
    consts = make_consts(L)
    wmap = {n: np.ascontiguousarray(np.asarray(inputs[n], dtype=np.float32)) for n in WNAMES}
    in_maps = []
    for ci in range(ncores):
        xcore = np.zeros((NS, L, D), np.float32)
        for k, si in enumerate(assign[ci]):
            if si < nseq:
                xcore[k] = seqs[si]
        m = {"x": xcore}
        m.update(wmap)
        m.update(consts)
        in_maps.append(m)
    res = run_bass_kernel_spmd(nc, in_maps, core_ids=list(range(ncores)))
    outs = [None] * nseq
    for ci in range(ncores):
        y = res.results[ci]["y"]
        for k, si in enumerate(assign[ci]):
            if si < nseq:
                outs[si] = y[k]
    y_prompt = np.stack(outs[:xp.shape[0]], axis=0).astype(np.float32)
    y_sample = np.stack(outs[xp.shape[0]:], axis=0).astype(np.float32)
    return (y_prompt, y_sample)
```
